# Optimizing a Trainium2 kernel written in Bass

```python
import math
import jax, jax.numpy as jnp
from jax import lax
import numpy as np

D_MODEL = 1024
BATCH = 32
SEQ = 2048
DEPTH = 2

HEAD_DIM = 64
ATT_HEADS = 6
ATT_KV_HEADS = 2
ATT_GROUP = ATT_HEADS // ATT_KV_HEADS
ATT_DIM = ATT_HEADS * HEAD_DIM
ATT_KV_DIM = ATT_KV_HEADS * HEAD_DIM
WINDOW = 128
N_BUCKETS = 32
MAX_DISTANCE = 128
RWKV_HEADS = 6
RWKV_DIM = RWKV_HEADS * HEAD_DIM
DECAY_LORA = 32
ICLR_LORA = 32
GATE_LORA = 64
RWKV_COLS = 3 * RWKV_DIM + DECAY_LORA + ICLR_LORA + GATE_LORA
GN_EPS = 64e-5
SSM_GROUPS = 16
SSM_GROUP_CH = 16
SSM_DIM = SSM_GROUPS * SSM_GROUP_CH
SSM_STATE = 64
MIX_WIDTH = ATT_DIM + RWKV_DIM + SSM_DIM
IN_COLS = ATT_DIM + 2 * ATT_KV_DIM + RWKV_COLS + SSM_DIM
D_FF = 2816
RMS_EPS = 1e-6

kernel_name = "hybrid_parallel_heads_block"


def rmsnorm(x, g):
    xf = x.astype(jnp.float32)
    y = xf * lax.rsqrt(jnp.mean(xf * xf, axis=-1, keepdims=True) + RMS_EPS) * g.astype(jnp.float32)
    return y.astype(x.dtype)


def swiglu(x, w_gate, w_up, w_down):
    return (jax.nn.silu(x @ w_gate) * (x @ w_up)) @ w_down


def band_relative_buckets():
    qi = jnp.arange(WINDOW)[:, None]
    kj = jnp.arange(2 * WINDOW)[None, :]
    rel = qi + WINDOW - kj
    in_window = (rel >= 0) & (rel < WINDOW)
    n = jnp.maximum(rel, 0)
    max_exact = N_BUCKETS // 2
    nf = jnp.maximum(n, 1).astype(jnp.float32)
    large = max_exact + (jnp.log(nf / max_exact) / math.log(MAX_DISTANCE / max_exact)
                         * (N_BUCKETS - max_exact)).astype(jnp.int32)
    large = jnp.minimum(large, N_BUCKETS - 1)
    bucket = jnp.where(n < max_exact, n, large)
    return bucket, in_window


def sliding_window_sink_attention(q, k, v, sinks, rel_bias):
    Bsz, T, _ = q.shape
    nb = T // WINDOW
    qb = q.reshape(Bsz, nb, WINDOW, ATT_KV_HEADS, ATT_GROUP, HEAD_DIM)

    def band(z):
        zb = jnp.pad(z, ((0, 0), (WINDOW, 0), (0, 0))).reshape(Bsz, nb + 1, WINDOW, ATT_KV_HEADS, HEAD_DIM)
        return jnp.concatenate([zb[:, :-1], zb[:, 1:]], axis=2)

    kb, vb = band(k), band(v)
    s = jnp.einsum('bnqhgd,bnkhd->bhgnqk', qb, kb,
                   preferred_element_type=jnp.float32) * (HEAD_DIM ** -0.5)
    bucket, in_window = band_relative_buckets()
    bias = jnp.transpose(rel_bias.astype(jnp.float32)[bucket], (2, 0, 1))
    bias = bias.reshape(ATT_KV_HEADS, ATT_GROUP, 1, WINDOW, 2 * WINDOW)
    key_pos = (jnp.arange(nb)[:, None, None] - 1) * WINDOW + jnp.arange(2 * WINDOW)[None, None, :]
    mask = in_window[None] & (key_pos >= 0)
    s = jnp.where(mask, s + bias, -jnp.inf)
    sink = sinks.astype(jnp.float32).reshape(ATT_KV_HEADS, ATT_GROUP, 1, 1, 1)
    m = jnp.maximum(s.max(axis=-1, keepdims=True), sink)
    p = jnp.exp(s - m)
    p = p / (p.sum(axis=-1, keepdims=True) + jnp.exp(sink - m))
    o = jnp.einsum('bhgnqk,bnkhd->bnqhgd', p, vb.astype(jnp.float32))
    return o.reshape(Bsz, T, ATT_DIM)


def rwkv7_time_mix(p, mu, w0, w2, a0, a2, g2, k_k, k_a, r_k, gn_w, gn_b):
    Bsz, T, _ = p.shape
    p = p.astype(jnp.float32)
    prev = jnp.pad(p[:, :-1], ((0, 0), (1, 0), (0, 0)))
    p = p + (prev - p) * mu
    o1 = RWKV_DIM
    o2 = 2 * RWKV_DIM
    o3 = 3 * RWKV_DIM
    o4 = o3 + DECAY_LORA
    o5 = o4 + ICLR_LORA
    r, k, v, wd, ad, gd = jnp.split(p, [o1, o2, o3, o4, o5], axis=-1)
    w = -jax.nn.softplus(-(w0 + jnp.tanh(wd) @ w2)) - 0.5
    decay = jnp.exp(-jnp.exp(w))
    a = jax.nn.sigmoid(a0 + ad @ a2)
    g = jax.nn.sigmoid(gd) @ g2

    def heads(z):
        return z.reshape(Bsz, T, RWKV_HEADS, HEAD_DIM)

    kk = heads(k * k_k)
    kk = kk / jnp.maximum(jnp.sqrt(jnp.sum(kk * kk, axis=-1, keepdims=True)), 1e-12)
    k = k * (1.0 + (a - 1.0) * k_a)
    r_h, k_h, v_h, w_h, a_h = heads(r), heads(k), heads(v), heads(decay), heads(a)
    xs = tuple(jnp.moveaxis(z, 1, 0) for z in (r_h, w_h, k_h, v_h, -kk, kk * a_h))

    def step(S, inp):
        rt, wt, kt, vt, at, bt = inp
        sa = jnp.einsum('bhvk,bhk->bhv', S, at)
        S = S * wt[:, :, None, :] + sa[..., None] * bt[:, :, None, :] + vt[..., None] * kt[:, :, None, :]
        return S, jnp.einsum('bhvk,bhk->bhv', S, rt)

    S0 = jnp.zeros((Bsz, RWKV_HEADS, HEAD_DIM, HEAD_DIM), jnp.float32)
    _, y = lax.scan(step, S0, xs)
    y = jnp.moveaxis(y, 0, 1)
    mean = jnp.mean(y, axis=-1, keepdims=True)
    var = jnp.mean(jnp.square(y - mean), axis=-1, keepdims=True)
    y = ((y - mean) * lax.rsqrt(var + GN_EPS)).reshape(Bsz, T, RWKV_DIM) * gn_w + gn_b
    y = y + (jnp.sum(r_h * k_h * r_k, axis=-1, keepdims=True) * v_h).reshape(Bsz, T, RWKV_DIM)
    return y * g


def s5_ssm(u, lam_re, lam_im, log_dt, b_re, b_im, c_re, c_im, d, glu_w, glu_b):
    Bsz, T, _ = u.shape
    f32 = jnp.float32
    uf = u.astype(f32).reshape(Bsz, T, SSM_GROUPS, SSM_GROUP_CH)
    lr = jnp.minimum(lam_re.astype(f32), -1e-4)
    li = lam_im.astype(f32)
    dt = jnp.exp(log_dt.astype(f32))[:, None]
    mag = jnp.exp(lr * dt)
    abar_re = mag * jnp.cos(li * dt)
    abar_im = mag * jnp.sin(li * dt)
    den = lr * lr + li * li
    nr = abar_re - 1.0
    ni = abar_im
    fr = (nr * lr + ni * li) / den
    fi = (ni * lr - nr * li) / den
    b_re = b_re.astype(f32)
    b_im = b_im.astype(f32)
    bbar_re = fr[..., None] * b_re - fi[..., None] * b_im
    bbar_im = fr[..., None] * b_im + fi[..., None] * b_re
    bu_re = jnp.einsum('btgh,gph->btgp', uf, bbar_re)
    bu_im = jnp.einsum('btgh,gph->btgp', uf, bbar_im)
    a_re = jnp.broadcast_to(abar_re, (1, T, SSM_GROUPS, SSM_STATE))
    a_im = jnp.broadcast_to(abar_im, (1, T, SSM_GROUPS, SSM_STATE))

    def combine(e1, e2):
        a1r, a1i, b1r, b1i = e1
        a2r, a2i, b2r, b2i = e2
        return (a2r * a1r - a2i * a1i, a2r * a1i + a2i * a1r,
                a2r * b1r - a2i * b1i + b2r, a2r * b1i + a2i * b1r + b2i)

    _, _, xr, xi = lax.associative_scan(combine, (a_re, a_im, bu_re, bu_im), axis=1)
    y = (jnp.einsum('btgp,ghp->btgh', xr, c_re.astype(f32))
         - jnp.einsum('btgp,ghp->btgh', xi, c_im.astype(f32))
         + d.astype(f32) * uf)
    y = jax.nn.gelu(y.reshape(Bsz, T, SSM_DIM))
    return y * jax.nn.sigmoid(y @ glu_w.astype(f32) + glu_b.astype(f32))


def hybrid_mixer(xn, w_in, w_out, att_sinks, rel_bias,
                 rwkv_mu, rwkv_w0, rwkv_w2, rwkv_a0, rwkv_a2, rwkv_g2,
                 rwkv_k_k, rwkv_k_a, rwkv_r_k, rwkv_gn_w, rwkv_gn_b,
                 ssm_lambda_re, ssm_lambda_im, ssm_log_dt, ssm_b_re, ssm_b_im,
                 ssm_c_re, ssm_c_im, ssm_d, ssm_glu_w, ssm_glu_b):
    p = xn @ w_in
    c1 = ATT_DIM
    c2 = c1 + ATT_KV_DIM
    c3 = c2 + ATT_KV_DIM
    c4 = c3 + RWKV_COLS
    q, k, v, rw, su = jnp.split(p, [c1, c2, c3, c4], axis=-1)
    y_att = sliding_window_sink_attention(q, k, v, att_sinks, rel_bias)
    y_rwkv = rwkv7_time_mix(rw, rwkv_mu, rwkv_w0, rwkv_w2, rwkv_a0, rwkv_a2, rwkv_g2,
                            rwkv_k_k, rwkv_k_a, rwkv_r_k, rwkv_gn_w, rwkv_gn_b)
    y_ssm = s5_ssm(su, ssm_lambda_re, ssm_lambda_im, ssm_log_dt, ssm_b_re, ssm_b_im,
                   ssm_c_re, ssm_c_im, ssm_d, ssm_glu_w, ssm_glu_b)
    dt = xn.dtype
    y = jnp.concatenate([y_att.astype(dt), y_rwkv.astype(dt), y_ssm.astype(dt)], axis=-1)
    return y @ w_out


def setup_inputs(seed: int = 0) -> dict:
    key = jax.random.key(seed)
    ks = iter(jax.random.split(key, 64))
    f32 = jnp.float32
    L = DEPTH

    def nrm(shape, scale):
        return scale * jax.random.normal(next(ks), shape, f32)

    def gain(shape):
        return 1.0 + 0.02 * jax.random.normal(next(ks), shape, f32)

    x = jax.random.normal(next(ks), (BATCH, SEQ, D_MODEL), f32)
    rel_bias = nrm((N_BUCKETS, ATT_HEADS), 0.2)
    ln_pre_ffn1 = gain((L, D_MODEL))
    ln_post_ffn1 = gain((L, D_MODEL))
    ffn1_w_gate = nrm((L, D_MODEL, D_FF), D_MODEL ** -0.5)
    ffn1_w_up = nrm((L, D_MODEL, D_FF), D_MODEL ** -0.5)
    ffn1_w_down = nrm((L, D_FF, D_MODEL), D_FF ** -0.5)
    ln_pre_mix = gain((L, D_MODEL))
    ln_post_mix = gain((L, D_MODEL))
    w_in = nrm((L, D_MODEL, IN_COLS), D_MODEL ** -0.5)
    w_out = nrm((L, MIX_WIDTH, D_MODEL), MIX_WIDTH ** -0.5)
    att_sinks = nrm((L, ATT_HEADS), 0.5)
    rwkv_mu = jax.random.uniform(next(ks), (L, RWKV_COLS), f32)
    ratio = jnp.arange(RWKV_DIM, dtype=f32) / (RWKV_DIM - 1)
    decay_speed = -7.0 + 5.0 * ratio ** (0.85 + 0.5 ** 0.5)
    rwkv_w0 = jnp.broadcast_to(decay_speed + 0.5, (L, RWKV_DIM)) + nrm((L, RWKV_DIM), 0.01)
    rwkv_w2 = nrm((L, DECAY_LORA, RWKV_DIM), 0.1 * DECAY_LORA ** -0.5)
    rwkv_a0 = nrm((L, RWKV_DIM), 0.1)
    rwkv_a2 = nrm((L, ICLR_LORA, RWKV_DIM), 0.5 * ICLR_LORA ** -0.5)
    rwkv_g2 = nrm((L, GATE_LORA, RWKV_DIM), GATE_LORA ** -0.5)
    rwkv_k_k = 0.85 + nrm((L, RWKV_DIM), 0.02)
    rwkv_k_a = 1.0 + nrm((L, RWKV_DIM), 0.02)
    rwkv_r_k = -0.04 + nrm((L, RWKV_HEADS, HEAD_DIM), 0.02)
    rwkv_gn_w = gain((L, RWKV_DIM))
    rwkv_gn_b = nrm((L, RWKV_DIM), 0.01)
    ssm_lambda_re = -0.5 + nrm((L, SSM_GROUPS, SSM_STATE), 0.01)
    ssm_lambda_im = (jnp.broadcast_to(math.pi * jnp.arange(SSM_STATE, dtype=f32), (L, SSM_GROUPS, SSM_STATE))
                     + nrm((L, SSM_GROUPS, SSM_STATE), 0.01))
    ssm_log_dt = jax.random.uniform(next(ks), (L, SSM_GROUPS), f32,
                                    minval=math.log(0.001), maxval=math.log(0.1))
    ssm_b_re = nrm((L, SSM_GROUPS, SSM_STATE, SSM_GROUP_CH), (2 * SSM_GROUP_CH) ** -0.5)
    ssm_b_im = nrm((L, SSM_GROUPS, SSM_STATE, SSM_GROUP_CH), (2 * SSM_GROUP_CH) ** -0.5)
    ssm_c_re = nrm((L, SSM_GROUPS, SSM_GROUP_CH, SSM_STATE), SSM_STATE ** -0.5)
    ssm_c_im = nrm((L, SSM_GROUPS, SSM_GROUP_CH, SSM_STATE), SSM_STATE ** -0.5)
    ssm_d = nrm((L, SSM_GROUPS, SSM_GROUP_CH), 1.0)
    ssm_glu_w = nrm((L, SSM_DIM, SSM_DIM), SSM_DIM ** -0.5)
    ssm_glu_b = nrm((L, SSM_DIM), 0.01)
    ln_pre_ffn2 = gain((L, D_MODEL))
    ln_post_ffn2 = gain((L, D_MODEL))
    ffn2_w_gate = nrm((L, D_MODEL, D_FF), D_MODEL ** -0.5)
    ffn2_w_up = nrm((L, D_MODEL, D_FF), D_MODEL ** -0.5)
    ffn2_w_down = nrm((L, D_FF, D_MODEL), D_FF ** -0.5)
    return {
        "x": x, "rel_bias": rel_bias,
        "ln_pre_ffn1": ln_pre_ffn1, "ln_post_ffn1": ln_post_ffn1,
        "ffn1_w_gate": ffn1_w_gate, "ffn1_w_up": ffn1_w_up, "ffn1_w_down": ffn1_w_down,
        "ln_pre_mix": ln_pre_mix, "ln_post_mix": ln_post_mix,
        "w_in": w_in, "w_out": w_out, "att_sinks": att_sinks,
        "rwkv_mu": rwkv_mu, "rwkv_w0": rwkv_w0, "rwkv_w2": rwkv_w2,
        "rwkv_a0": rwkv_a0, "rwkv_a2": rwkv_a2, "rwkv_g2": rwkv_g2,
        "rwkv_k_k": rwkv_k_k, "rwkv_k_a": rwkv_k_a, "rwkv_r_k": rwkv_r_k,
        "rwkv_gn_w": rwkv_gn_w, "rwkv_gn_b": rwkv_gn_b,
        "ssm_lambda_re": ssm_lambda_re, "ssm_lambda_im": ssm_lambda_im, "ssm_log_dt": ssm_log_dt,
        "ssm_b_re": ssm_b_re, "ssm_b_im": ssm_b_im, "ssm_c_re": ssm_c_re, "ssm_c_im": ssm_c_im,
        "ssm_d": ssm_d, "ssm_glu_w": ssm_glu_w, "ssm_glu_b": ssm_glu_b,
        "ln_pre_ffn2": ln_pre_ffn2, "ln_post_ffn2": ln_post_ffn2,
        "ffn2_w_gate": ffn2_w_gate, "ffn2_w_up": ffn2_w_up, "ffn2_w_down": ffn2_w_down,
    }


def reference(x, rel_bias,
              ln_pre_ffn1, ln_post_ffn1, ffn1_w_gate, ffn1_w_up, ffn1_w_down,
              ln_pre_mix, ln_post_mix, w_in, w_out, att_sinks,
              rwkv_mu, rwkv_w0, rwkv_w2, rwkv_a0, rwkv_a2, rwkv_g2,
              rwkv_k_k, rwkv_k_a, rwkv_r_k, rwkv_gn_w, rwkv_gn_b,
              ssm_lambda_re, ssm_lambda_im, ssm_log_dt, ssm_b_re, ssm_b_im, ssm_c_re, ssm_c_im,
              ssm_d, ssm_glu_w, ssm_glu_b,
              ln_pre_ffn2, ln_post_ffn2, ffn2_w_gate, ffn2_w_up, ffn2_w_down):
    h = x
    for l in range(DEPTH):
        f = swiglu(rmsnorm(h, ln_pre_ffn1[l]), ffn1_w_gate[l], ffn1_w_up[l], ffn1_w_down[l])
        h = h + 0.5 * rmsnorm(f, ln_post_ffn1[l])
        m = hybrid_mixer(rmsnorm(h, ln_pre_mix[l]), w_in[l], w_out[l], att_sinks[l], rel_bias,
                         rwkv_mu[l], rwkv_w0[l], rwkv_w2[l], rwkv_a0[l], rwkv_a2[l], rwkv_g2[l],
                         rwkv_k_k[l], rwkv_k_a[l], rwkv_r_k[l], rwkv_gn_w[l], rwkv_gn_b[l],
                         ssm_lambda_re[l], ssm_lambda_im[l], ssm_log_dt[l], ssm_b_re[l], ssm_b_im[l],
                         ssm_c_re[l], ssm_c_im[l], ssm_d[l], ssm_glu_w[l], ssm_glu_b[l])
        h = h + rmsnorm(m, ln_post_mix[l])
        f = swiglu(rmsnorm(h, ln_pre_ffn2[l]), ffn2_w_gate[l], ffn2_w_up[l], ffn2_w_down[l])
        h = h + 0.5 * rmsnorm(f, ln_post_ffn2[l])
    return h
```

```python
import contextlib
import numpy as np
import concourse.bass as bass
import concourse.mybir as mybir
from concourse.bass_utils import run_bass_kernel_spmd

F32 = mybir.dt.float32
BF16 = mybir.dt.bfloat16
AF = mybir.ActivationFunctionType
ALU = mybir.AluOpType
AX = mybir.AxisListType

PE, ACT, DVE, POOL, SP = "tensor", "scalar", "vector", "gpsimd", "sync"
ENGS = [PE, ACT, DVE, POOL, SP]
SEM_ROLL = 30000
NDMASEM = 6
MIN_GAP = 3

T = 2048
D = 1024
NCH = 8
DFF = 2816
NF = 22
EPS = 1e-6


class Buf:
    __slots__ = ("name", "last_w", "readers")

    def __init__(self, name=""):
        self.name = name
        self.last_w = None
        self.readers = []


class Prog:
    def __init__(self, nc):
        self.nc = nc
        self.ops = {e: [] for e in ENGS}
        self.nsem = 0
        self.cur = {}
        self.cnt = {}
        for e in ENGS:
            self.cur[e] = self._newsem()
            self.cnt[e] = 0
        self.dsem = {e: [self._newsem() for _ in range(NDMASEM)] for e in (SP, ACT, POOL)}
        self.dcnt = {e: 0 for e in (SP, ACT, POOL)}
        self.waited = {e: {} for e in ENGS}
        self.nissued = {e: 0 for e in ENGS}
        self.final_dma = []

    def _newsem(self):
        k = self.nsem
        self.nsem += 1
        return k

    def _need(self, eng, dep, waits):
        if dep is None:
            return
        k, v, src = dep[0], dep[1], dep[2]
        if src == eng and eng != PE and len(dep) > 3 and self.nissued[eng] - dep[3] >= MIN_GAP:
            return
        if self.waited[eng].get(k, 0) >= v:
            return
        if waits.get(k, 0) < v:
            waits[k] = v

    def op(self, eng, fn, reads=(), writes=(), inc=True):
        waits = {}
        for b in reads:
            self._need(eng, b.last_w, waits)
        for b in writes:
            if b.last_w is not None and b.last_w[2] != eng:
                self._need(eng, b.last_w, waits)
            for r in b.readers:
                if r[2] != eng:
                    self._need(eng, r, waits)
        for k, v in waits.items():
            self.waited[eng][k] = v
        if self.cnt[eng] >= SEM_ROLL:
            self.cur[eng] = self._newsem()
            self.cnt[eng] = 0
        self.nissued[eng] += 1
        if inc:
            self.cnt[eng] += 1
            tok = (self.cur[eng], self.cnt[eng], eng, self.nissued[eng])
            incspec = (self.cur[eng], 1, self.cnt[eng])
        else:
            tok = (self.cur[eng], self.cnt[eng] + 1, eng, self.nissued[eng])
            incspec = None
        self.ops[eng].append((fn, list(waits.items()), incspec))
        for b in reads:
            b.readers.append(tok)
        for b in writes:
            b.last_w = tok
            b.readers = []
        return tok

    def dma(self, q, out, in_, reads=(), writes=(), final=False):
        waits = {}
        for b in reads:
            self._need(q, b.last_w, waits)
        for b in writes:
            self._need(q, b.last_w, waits)
            for r in b.readers:
                self._need(q, r, waits)
        i = self.dcnt[q]
        self.dcnt[q] += 1
        sem = self.dsem[q][i % NDMASEM]
        rnd = i // NDMASEM
        if rnd > 0:
            self._need(q, (sem, 16 * rnd, "dma"), waits)
        for k, v in waits.items():
            self.waited[q][k] = v
        tok = (sem, 16 * (rnd + 1), "dma")

        def fn(e, out=out, in_=in_):
            return e.dma_start(out=out, in_=in_)
        self.ops[q].append((fn, list(waits.items()), (sem, 16)))
        for b in reads:
            b.readers.append(tok)
        for b in writes:
            b.last_w = tok
            b.readers = []
        if final:
            self.final_dma.append((q, tok))
        return tok

    def barrier(self):
        targets = {}
        for e in ENGS:
            if self.cnt[e] > 0:
                targets[self.cur[e]] = (self.cnt[e], e)
        for q in self.dsem:
            for i in range(self.dcnt[q]):
                targets[self.dsem[q][i % NDMASEM]] = (16 * (i // NDMASEM + 1), "dma")
        for e in ENGS:
            waits = {}
            for k, (v, src) in targets.items():
                if self.waited[e].get(k, 0) < v:
                    waits[k] = v
                    self.waited[e][k] = v
            if waits:
                self.ops[e].append((None, list(waits.items()), None))

    def emit(self):
        nc = self.nc
        for q, tok in self.final_dma:
            self.ops[q].append((None, [(tok[0], tok[1])], None))
        dma_sems = set()
        for q in self.dsem:
            dma_sems.update(self.dsem[q])
        needed = {}
        for e in ENGS:
            for fn, waits, incspec in self.ops[e]:
                for k, v in waits:
                    if k not in dma_sems:
                        needed.setdefault(k, set()).add(v)
        remap = {}
        newcnt = {}
        for e in ENGS:
            for fn, waits, incspec in self.ops[e]:
                if incspec is not None and len(incspec) == 3:
                    k, _, v = incspec
                    if v in needed.get(k, ()):
                        newcnt[k] = newcnt.get(k, 0) + 1
                        remap[(k, v)] = newcnt[k]
        with contextlib.ExitStack() as st:
            sems = [st.enter_context(nc.semaphore("s%d" % i)) for i in range(self.nsem)]
            block = st.enter_context(nc.Block())

            def mk(engname):
                lst = self.ops[engname]

                def body(e):
                    for fn, waits, incspec in lst:
                        for k, v in waits:
                            if k in dma_sems:
                                e.wait_ge(sems[k], v)
                            else:
                                e.wait_ge(sems[k], remap[(k, v)])
                        if fn is None:
                            continue
                        ins = fn(e)
                        if incspec is not None:
                            if len(incspec) == 3:
                                if (incspec[0], incspec[2]) in remap:
                                    ins.then_inc(sems[incspec[0]], 1)
                            else:
                                ins.then_inc(sems[incspec[0]], incspec[1])
                return body
            for engname in ENGS:
                if self.ops[engname]:
                    getattr(block, engname)(mk(engname))
        return nc


class KB:
    def __init__(self, nc, st):
        self.nc = nc
        self.st = st
        self.P = Prog(nc)
        self.psb = []
        for i in range(8):
            t = st.enter_context(nc.psum_tensor("psb%d" % i, [128, 512], F32))
            self.psb.append((t, Buf("ps%d" % i)))

    def sb(self, name, shape, dt=F32):
        return self.st.enter_context(self.nc.sbuf_tensor(name, list(shape), dt))

    def mm(self, out, lhsT, rhs, start, stop, reads, writes, inc=True):
        self.P.op(PE, lambda e: e.matmul(out, lhsT=lhsT, rhs=rhs, start=start, stop=stop), reads, writes, inc)

    def tr(self, out, in_, ident, reads, writes, inc=True):
        self.P.op(PE, lambda e: e.transpose(out, in_, ident), reads, writes, inc)

    def act(self, out, in_, func, reads, writes, bias=None, scale=None, eng=ACT):
        kw = {}
        if bias is not None:
            kw["bias"] = bias
        if scale is not None:
            kw["scale"] = scale
        self.P.op(ACT, lambda e: e.activation(out=out, in_=in_, func=func, **kw), reads, writes)

    def tt(self, out, in0, in1, op, reads, writes, eng=DVE):
        self.P.op(eng, lambda e: e.tensor_tensor(out=out, in0=in0, in1=in1, op=op), reads, writes)

    def ts(self, out, in0, s1, s2, op0, op1, reads, writes, eng=DVE):
        if op1 is None:
            self.P.op(eng, lambda e: e.tensor_scalar(out=out, in0=in0, scalar1=s1, scalar2=None, op0=op0), reads, writes)
        else:
            self.P.op(eng, lambda e: e.tensor_scalar(out=out, in0=in0, scalar1=s1, scalar2=s2, op0=op0, op1=op1), reads, writes)

    def stt(self, out, in0, scalar, in1, op0, op1, reads, writes, eng=DVE):
        self.P.op(eng, lambda e: e.scalar_tensor_tensor(out=out, in0=in0, scalar=scalar, in1=in1, op0=op0, op1=op1), reads, writes)

    def cp(self, out, in_, reads, writes, eng=DVE):
        if eng == ACT:
            self.P.op(ACT, lambda e: e.copy(out=out, in_=in_), reads, writes)
        else:
            self.P.op(eng, lambda e: e.tensor_copy(out=out, in_=in_), reads, writes)

    def memset(self, ap, val, writes, eng=DVE):
        self.P.op(eng, lambda e: e.memset(ap, val), (), writes)


NOC = 17
AW = 32000


ALL_PARTS = ("conv", "s5prep", "ffn", "inproj", "attn", "ssm", "rwkv", "outproj")


def build(nseq, stop_after=99, n_layers=2, debug=None, parts=ALL_PARTS):
    nc = bass.Bass("TRN2", target_bir_lowering=False)
    xT = nc.dram_tensor("xT", [nseq, NCH, 128, T], F32, kind="ExternalInput").ap()
    outT = nc.dram_tensor("outT", [nseq, NCH, 128, T], F32, kind="ExternalOutput").ap()
    wgu = nc.dram_tensor("wgu", [4, 2, DFF, D], F32, kind="ExternalInput").ap()
    wdn = nc.dram_tensor("wdn", [4, D, DFF], F32, kind="ExternalInput").ap()
    win = nc.dram_tensor("win", [2, NOC * 128, D], F32, kind="ExternalInput").ap()
    wout = nc.dram_tensor("wout", [2, D, D], F32, kind="ExternalInput").ap()
    gains = nc.dram_tensor("gains", [128, 12, NCH], F32, kind="ExternalInput").ap()
    ident_d = nc.dram_tensor("ident", [128, 128], F32, kind="ExternalInput").ap()
    biasT_d = nc.dram_tensor("biasT", [128, 2, 6, 128], F32, kind="ExternalInput").ap()
    maskT_d = nc.dram_tensor("maskT", [128, 2, 128], F32, kind="ExternalInput").ap()
    sinks_d = nc.dram_tensor("sinks", [128, 2, 6], F32, kind="ExternalInput").ap()
    wgu_b = nc.dram_tensor("wgu_b", [4, 2, DFF, D], BF16, kind="Internal").ap()
    wdn_b = nc.dram_tensor("wdn_b", [4, D, DFF], BF16, kind="Internal").ap()
    win_b = nc.dram_tensor("win_b", [2, NOC * 128, D], BF16, kind="Internal").ap()
    wout_b = nc.dram_tensor("wout_b", [2, D, D], BF16, kind="Internal").ap()
    pT = nc.dram_tensor("pT", [NOC, 128, T], F32, kind="Internal").ap()
    rwp_d = nc.dram_tensor("rwp", [128, 2, 31], F32, kind="ExternalInput").ap()
    lw_d = nc.dram_tensor("lw", [128, 2, 384], F32, kind="ExternalInput").ap()
    mU_d = nc.dram_tensor("mU", [128, 2, 128], F32, kind="ExternalInput").ap()
    mL_d = nc.dram_tensor("mL", [128, 128], F32, kind="ExternalInput").ap()
    BD_d = nc.dram_tensor("BD", [128, 128], F32, kind="ExternalInput").ap()
    rowm_d = nc.dram_tensor("rowm", [128, 2], F32, kind="ExternalInput").ap()
    s5p_d = nc.dram_tensor("s5p", [128, 2, 3, 8], F32, kind="ExternalInput").ap()
    bb_d = nc.dram_tensor("bb", [128, 2, 2, 8, 128], F32, kind="ExternalInput").ap()
    cc_d = nc.dram_tensor("cc", [128, 2, 2, 8, 128], F32, kind="ExternalInput").ap()
    s5v_d = nc.dram_tensor("s5v", [128, 2, 2, 2], F32, kind="ExternalInput").ap()
    gluw_d = nc.dram_tensor("gluw", [128, 2, 2, 2, 128], F32, kind="ExternalInput").ap()
    tabs = nc.dram_tensor("tabs", [2, 8, 128, 4, 512], F32, kind="Internal").ap()
    dbg = None
    if debug is not None:
        dbg = nc.dram_tensor("dbg", [nseq, NCH, 128, T], F32, kind="ExternalOutput").ap()

    with contextlib.ExitStack() as st:
        kb = KB(nc, st)
        P = kb.P
        h = kb.sb("h", [128, NCH, T], F32)
        hB = [Buf("h%d" % i) for i in range(4)]
        ones_b = kb.sb("ones_b", [128, 128], BF16)
        onesB = Buf("ones")
        kb.memset(ones_b[:], 1.0, [onesB])
        ident = kb.sb("identsb", [128, 128], F32)
        identB = Buf("ident")
        P.dma(SP, ident[:], ident_d[:, :], writes=[identB])
        g_sb = kb.sb("g_sb", [128, 12, NCH], F32)
        gB = Buf("g")
        P.dma(SP, g_sb[:], gains[:, :, :], writes=[gB])
        g32 = kb.sb("g32", [128, 12, NCH], F32)
        g32B = Buf("g32")
        for idx in range(12):
            k6 = idx % 6
            sc = 16.0 if k6 in (1, 5) else 32.0
            kb.ts(g32[:, idx, :], g_sb[:, idx, :], sc, None, ALU.mult, None, [gB], [g32B])
        epsb = kb.sb("epsb", [128, 1], F32)
        epsB = Buf("epsb")
        kb.memset(epsb[:], float(D * EPS), [epsB])
        biasT = kb.sb("biasT_sb", [128, 2, 6, 128], F32)
        biasB = Buf("biasT")
        maskT = kb.sb("maskT_sb", [128, 2, 128], F32)
        maskB = Buf("maskT")
        P.dma(SP, biasT[:], biasT_d[:, :, :, :], writes=[biasB])
        P.dma(SP, maskT[:], maskT_d[:, :, :], writes=[maskB])
        for hh in range(6):
            kb.tt(biasT[:, :, hh, :], biasT[:, :, hh, :], maskT[:], ALU.add, [biasB, maskB], [biasB])
        esink = kb.sb("esink", [128, 2, 6], F32)
        esinkB = Buf("esink")
        P.dma(SP, esink[:], sinks_d[:, :, :], writes=[esinkB])
        kb.act(esink[:], esink[:], AF.Exp, [esinkB], [esinkB])

        rwp = kb.sb("rwp_sb", [128, 2, 31], F32); rwpB = Buf("rwp")
        P.dma(SP, rwp[:], rwp_d[:, :, :], writes=[rwpB])
        lw = kb.sb("lw_sb", [128, 2, 384], F32); lwB = Buf("lw")
        P.dma(SP, lw[:], lw_d[:, :, :], writes=[lwB])
        mU = kb.sb("mU_sb", [128, 2, 128], F32); mUB = Buf("mU")
        P.dma(SP, mU[:], mU_d[:, :, :], writes=[mUB])
        mL = kb.sb("mL_sb", [128, 128], F32); mLB = Buf("mL")
        P.dma(SP, mL[:], mL_d[:, :], writes=[mLB])
        BD = kb.sb("BD_sb", [128, 128], F32); BDB = Buf("BD")
        P.dma(SP, BD[:], BD_d[:, :], writes=[BDB])
        rowm = kb.sb("rowm_sb", [128, 2], F32); rowmB = Buf("rowm")
        P.dma(SP, rowm[:], rowm_d[:, :], writes=[rowmB])
        rwd = kb.sb("rwd", [128, 2, 16], F32); rwdB = Buf("rwd")
        kb.ts(rwd[:, :, 0:10], rwp[:, :, 0:10], -1.0, 1.0, ALU.mult, ALU.add, [rwpB], [rwdB])
        kb.ts(rwd[:, :, 10:13], rwp[:, :, 10:13], -1.0, None, ALU.mult, None, [rwpB], [rwdB])
        kb.ts(rwd[:, :, 13:16], rwp[:, :, 19:22], -1.0, 1.0, ALU.mult, ALU.add, [rwpB], [rwdB])
        cst = kb.sb("cst", [128, 4], F32); cstB = Buf("cst")
        kb.memset(cst[:, 0:1], 1.0, [cstB])
        kb.memset(cst[:, 1:2], -0.5, [cstB])
        kb.memset(cst[:, 2:3], 1e-24, [cstB])
        kb.memset(cst[:, 3:4], 64e-5, [cstB])
        ones128 = kb.sb("ones128", [128, 128], F32); ones128B = Buf("ones128")
        kb.memset(ones128[:], 1.0, [ones128B])
        s5p = kb.sb("s5p_sb", [128, 2, 3, 8], F32); s5pB = Buf("s5p")
        P.dma(SP, s5p[:], s5p_d[:, :, :, :], writes=[s5pB])
        s5v = kb.sb("s5v_sb", [128, 2, 2, 2], F32); s5vB = Buf("s5v")
        P.dma(SP, s5v[:], s5v_d[:, :, :, :], writes=[s5vB])
        gluw = kb.sb("gluw_sb", [128, 2, 2, 2, 128], F32); gluwB = Buf("gluw")
        P.dma(SP, gluw[:], gluw_d[:, :, :, :, :], writes=[gluwB])
        rho = kb.sb("rho", [128, 2, 8], F32); rhoB = Buf("rho")
        hpi = kb.sb("hpi", [128, 1], F32); hpiB = Buf("hpi")
        kb.memset(hpi[:], float(np.pi / 2), [hpiB])

        wB = Buf("wconv")
        conv_list = []
        for fi in range(2 * n_layers if "conv" in parts else 0):
            for m in range(2):
                for q4 in range(4):
                    r0 = q4 * (DFF // 4)
                    conv_list.append((wgu_b[fi, m, r0:r0 + DFF // 4, :], wgu[fi, m, r0:r0 + DFF // 4, :]))
            for q4 in range(4):
                r0 = q4 * (D // 4)
                conv_list.append((wdn_b[fi, r0:r0 + D // 4, :], wdn[fi, r0:r0 + D // 4, :]))
        for l in range(n_layers if "conv" in parts else 0):
            for q4 in range(NOC):
                conv_list.append((win_b[l, q4 * 128:(q4 + 1) * 128, :], win[l, q4 * 128:(q4 + 1) * 128, :]))
            for q4 in range(4):
                conv_list.append((wout_b[l, q4 * 256:(q4 + 1) * 256, :], wout[l, q4 * 256:(q4 + 1) * 256, :]))
        NFIRST = 12 if "ffn" in parts else len(conv_list)
        for o_, i_ in conv_list[:NFIRST]:
            P.dma(POOL, o_, i_, writes=[wB])
        conv_rest = conv_list[NFIRST:]
        P.barrier()
        convB = Buf("conv")

        arena = kb.sb("arena", [128, AW], F32)

        class Carver:
            def __init__(self):
                self.off = 0

            def f32(self, shape):
                n = int(np.prod(shape))
                ap = arena[:, self.off:self.off + n]
                self.off += n
                assert self.off <= AW, self.off
                return self._shape(ap, shape)

            def bf16(self, shape):
                n = int(np.prod(shape))
                nw = (n + 1) // 2
                ap = arena[:, self.off:self.off + nw].bitcast(BF16)
                self.off += nw
                assert self.off <= AW, self.off
                if 2 * nw != n:
                    ap = ap[:, 0:n]
                return self._shape(ap, shape)

            @staticmethod
            def _shape(ap, shape):
                if len(shape) == 1:
                    return ap
                if len(shape) == 2:
                    return ap.rearrange("p (a b) -> p a b", b=shape[1])
                if len(shape) == 3:
                    return ap.rearrange("p (a b c) -> p a b c", b=shape[1], c=shape[2])
                raise ValueError(shape)

        def s5_prep(l):
            cp_ = Carver()
            sm = [cp_.f32([8]) for i in range(16)]
            smB = Buf("s5small")
            Ec = cp_.f32([8, 512]); Es = cp_.f32([8, 512]); EB = Buf("E")
            Tc = cp_.f32([8, 512]); Tn = cp_.f32([8, 512]); TB = Buf("T")
            t1 = cp_.f32([8, 512]); t2 = cp_.f32([8, 512]); tB = Buf("t12")
            lr, dt, th, c_, s_, c2, s2, u1, u2, nr, fr, fi, den, li = sm[:14]
            R, W = [s5pB, smB], [smB]
            kb.ts(lr, s5p[:, l, 0, :], -1e-4, None, ALU.min, None, R, W)
            kb.cp(li, s5p[:, l, 1, :], R, W)
            kb.act(dt, s5p[:, l, 2, :], AF.Exp, R, W)
            kb.tt(u1, lr, dt, ALU.mult, R, W)
            kb.act(rho[:, l, :], u1, AF.Exp, R, [rhoB])
            kb.tt(th, li, dt, ALU.mult, R, W)
            kb.act(s_, th, AF.Sin, R, W, scale=1.0 / 64)
            kb.act(c_, th, AF.Sin, R + [hpiB], W, scale=1.0 / 64, bias=hpi[:, 0:1])
            for it in range(6):
                kb.tt(u1, c_, c_, ALU.mult, R, W)
                kb.tt(u2, s_, s_, ALU.mult, R, W)
                kb.tt(s2, c_, s_, ALU.mult, R, W)
                kb.tt(c2, u1, u2, ALU.subtract, R, W)
                kb.ts(s_, s2, 2.0, None, ALU.mult, None, R, W)
                kb.cp(c_, c2, R, W)
            kb.tt(u1, rho[:, l, :], c_, ALU.mult, R + [rhoB], W)
            kb.tt(u2, rho[:, l, :], s_, ALU.mult, R + [rhoB], W)
            kb.ts(nr, u1, -1.0, None, ALU.add, None, R, W)
            kb.tt(den, lr, lr, ALU.mult, R, W)
            kb.tt(c2, li, li, ALU.mult, R, W)
            kb.tt(den, den, c2, ALU.add, R, W)
            kb.P.op(DVE, lambda e: e.reciprocal(out=den, in_=den), R, W)
            kb.tt(c2, nr, lr, ALU.mult, R, W)
            kb.tt(s2, u2, li, ALU.mult, R, W)
            kb.tt(fr, c2, s2, ALU.add, R, W)
            kb.tt(fr, fr, den, ALU.mult, R, W)
            kb.tt(c2, u2, lr, ALU.mult, R, W)
            kb.tt(s2, nr, li, ALU.mult, R, W)
            kb.tt(fi, c2, s2, ALU.subtract, R, W)
            kb.tt(fi, fi, den, ALU.mult, R, W)
            RE = [smB, EB, tB]
            kb.cp(Ec[:, :, 0:1], c_.unsqueeze(2), RE, [EB])
            kb.cp(Es[:, :, 0:1], s_.unsqueeze(2), RE, [EB])
            n = 1
            while n < 512:
                bc = Ec[:, :, n - 1:n].to_broadcast([128, 8, n])
                bs = Es[:, :, n - 1:n].to_broadcast([128, 8, n])
                kb.tt(t1[:, :, 0:n], Ec[:, :, 0:n], bc, ALU.mult, RE, [tB])
                kb.tt(t2[:, :, 0:n], Es[:, :, 0:n], bs, ALU.mult, RE, [tB])
                kb.tt(Ec[:, :, n:2 * n], t1[:, :, 0:n], t2[:, :, 0:n], ALU.subtract, RE, [EB])
                kb.tt(t1[:, :, 0:n], Ec[:, :, 0:n], bs, ALU.mult, RE, [tB])
                kb.tt(t2[:, :, 0:n], Es[:, :, 0:n], bc, ALU.mult, RE, [tB])
                kb.tt(Es[:, :, n:2 * n], t1[:, :, 0:n], t2[:, :, 0:n], ALU.add, RE, [EB])
                n *= 2
            frb = fr.unsqueeze(2).to_broadcast([128, 8, 512])
            fib = fi.unsqueeze(2).to_broadcast([128, 8, 512])
            kb.tt(t1, Ec, frb, ALU.mult, RE, [tB])
            kb.tt(t2, Es, fib, ALU.mult, RE, [tB])
            kb.tt(Tc, t1, t2, ALU.add, RE, [TB])
            kb.tt(t1, Ec, fib, ALU.mult, RE, [tB])
            kb.tt(t2, Es, frb, ALU.mult, RE, [tB])
            kb.tt(Tn, t1, t2, ALU.subtract, RE, [TB])
            for j in range(8):
                for a, src in enumerate((Ec, Es, Tc, Tn)):
                    P.dma(SP, tabs[l, j, :, a, :], src[:, j, :], reads=[EB, TB])
            P.barrier()

        for l in range(n_layers if "s5prep" in parts else 0):
            s5_prep(l)

        cv = Carver()
        sq = cv.bf16([NCH, 512]); sqB = Buf("sq")
        rstd = cv.f32([512]); rstdB = Buf("rstd")
        rstd0 = cv.f32([512]); rstd0B = Buf("rstd0")
        xn = cv.bf16([NCH, 512]); xnB = Buf("xn")
        xnb_ = cv.bf16([NCH, 512]); xnbB = Buf("xnb")
        xn2 = [xn, xnb_]; xn2B = [xnB, xnbB]
        sqp = cv.bf16([NCH, 512]); sqpB = Buf("sqp")
        rstdp = cv.f32([512]); rstdpB = Buf("rstdp")
        rstd0p = cv.f32([512]); rstd0pB = Buf("rstd0p")
        actb = cv.bf16([NF, 512]); actB = [Buf("act%d" % f) for f in range(NF)]
        fout = cv.f32([NCH, 512]); foutB = Buf("fout")
        sg = [cv.f32([512]) for i in range(2)]; sgB = [Buf("sg%d" % i) for i in range(2)]
        tmp = [cv.f32([512]) for i in range(2)]; tmpB = [Buf("tmp%d" % i) for i in range(2)]
        NWS = 3
        wgus = [cv.bf16([2, D]) for i in range(NWS)]; wgusB = [Buf("wgus%d" % i) for i in range(NWS)]
        wds = [cv.bf16([DFF]) for i in range(2)]; wdsB = [Buf("wds%d" % i) for i in range(2)]
        cnt = {"w": 0, "d": 0, "g": 0, "o": 0, "s": 0, "t": 0, "e": 0}

        def rms_rstd(src_fn, srcB, rstd_, rstdB_, rstd0_, rstd0B_, nchunks=NCH):
            ps, psB = kb.psb[0]
            for c in range(nchunks):
                kb.mm(ps[:], ones_b[:], src_fn(c), c == 0, c == nchunks - 1, [onesB, srcB], [psB], inc=(c == nchunks - 1))
            kb.act(rstd0_, ps[:], AF.Sqrt, [psB, epsB], [rstd0B_], bias=epsb[:, 0:1])
            kb.P.op(DVE, lambda e: e.reciprocal(out=rstd_, in_=rstd0_), [rstd0B_], [rstdB_])

        def ffn(fi, gpre, gpost):
            def prenorm(tt):
                t0 = tt * 512
                hb = hB[tt]
                xb, xbB = xn2[tt % 2], xn2B[tt % 2]
                kb.act(sqp, h[:, :, t0:t0 + 512], AF.Square, [hb], [sqpB])
                rms_rstd(lambda c: sqp[:, c, :], sqpB, rstdp, rstdpB, rstd0p, rstd0pB)
                for c in range(NCH):
                    kb.stt(xb[:, c, :], h[:, c, t0:t0 + 512], g32[:, gpre, c:c + 1], rstdp, ALU.mult, ALU.mult,
                           [hb, g32B, rstdpB], [xbB])

            def gateup(tt):
                xb, xbB = xn2[tt % 2], xn2B[tt % 2]
                for f in range(NF):
                    ws = cnt["w"] % NWS
                    cnt["w"] += 1
                    P.dma(SP, wgus[ws], wgu_b[fi, :, f * 128:(f + 1) * 128, :].rearrange("m p k -> p m k"),
                          reads=[convB], writes=[wgusB[ws]])
                    pg, pgB = kb.psb[1 + cnt["g"] % 2]
                    pu, puB = kb.psb[3 + cnt["g"] % 2]
                    cnt["g"] += 1
                    for c in range(NCH):
                        kb.mm(pg[:], wgus[ws][:, 0, c * 128:(c + 1) * 128], xb[:, c, :], c == 0, c == NCH - 1,
                              [wgusB[ws], xbB], [pgB], inc=(c == NCH - 1))
                    for c in range(NCH):
                        kb.mm(pu[:], wgus[ws][:, 1, c * 128:(c + 1) * 128], xb[:, c, :], c == 0, c == NCH - 1,
                              [wgusB[ws], xbB], [puB], inc=(c == NCH - 1))
                    si = cnt["s"] % 2
                    cnt["s"] += 1
                    kb.act(sg[si], pg[:], AF.Silu, [pgB], [sgB[si]])
                    kb.tt(actb[:, f, :], sg[si], pu[:], ALU.mult, [sgB[si], puB], [actB[f]])

            def down_post(tt):
                t0 = tt * 512
                hb = hB[tt]
                for dc in range(NCH):
                    di = cnt["d"] % 2
                    cnt["d"] += 1
                    P.dma(SP, wds[di], wdn_b[fi, dc * 128:(dc + 1) * 128, :], reads=[convB], writes=[wdsB[di]])
                    po, poB = kb.psb[5 + cnt["o"] % 2]
                    cnt["o"] += 1
                    for f in range(NF):
                        kb.mm(po[:], wds[di][:, f * 128:(f + 1) * 128], actb[:, f, :], f == 0, f == NF - 1,
                              [wdsB[di], actB[f]], [poB], inc=(f == NF - 1))
                    kb.cp(fout[:, dc, :], po[:], [poB], [foutB], eng=ACT)
                    kb.act(sq[:, dc, :], po[:], AF.Square, [poB], [sqB])
                rms_rstd(lambda c: sq[:, c, :], sqB, rstd, rstdB, rstd0, rstd0B)
                for c in range(NCH):
                    ti = cnt["t"] % 2
                    cnt["t"] += 1
                    kb.stt(tmp[ti], fout[:, c, :], g32[:, gpost, c:c + 1], rstd, ALU.mult, ALU.mult,
                           [foutB, g32B, rstdB], [tmpB[ti]])
                    kb.tt(h[:, c, t0:t0 + 512], h[:, c, t0:t0 + 512], tmp[ti], ALU.add, [hb, tmpB[ti]], [hb], eng=POOL)

            prenorm(0)
            for tt in range(4):
                gateup(tt)
                if tt < 3:
                    prenorm(tt + 1)
                down_post(tt)

        cm = Carver()
        ycat = cm.bf16([NCH, T]); ycatB = [Buf("ycat%d" % i) for i in range(NCH)]
        MB1 = cm.off
        xnT = cm.bf16([NCH, T]); xnTB = [Buf("xnT%d" % i) for i in range(4)]
        MB0 = cm.off

        pTB = [Buf("pT%d" % i) for i in range(NOC)]

        def mixer_inproj(l):
            cm.off = MB0
            msq = cm.bf16([NCH, 512]); msqB = Buf("msq")
            mr = cm.f32([512]); mrB = Buf("mr")
            mr0 = cm.f32([512]); mr0B = Buf("mr0")
            wsl = [cm.bf16([D]) for i in range(3)]; wslB = [Buf("wsl%d" % i) for i in range(3)]
            stg = [cm.f32([512]) for i in range(3)]; stgB = [Buf("stg%d" % i) for i in range(3)]
            gpre = 6 * l + 2
            for tt in range(4):
                t0 = tt * 512
                kb.act(msq, h[:, :, t0:t0 + 512], AF.Square, [hB[tt]], [msqB])
                rms_rstd(lambda c: msq[:, c, :], msqB, mr, mrB, mr0, mr0B)
                for c in range(NCH):
                    kb.stt(xnT[:, c, t0:t0 + 512], h[:, c, t0:t0 + 512], g32[:, gpre, c:c + 1], mr, ALU.mult, ALU.mult,
                           [hB[tt], g32B, mrB], [xnTB[tt]])
            k = 0
            for oc in range(NOC):
                if oc == 4:
                    continue
                wi = oc % 3
                P.dma(SP, wsl[wi], win_b[l, oc * 128:(oc + 1) * 128, :], reads=[convB], writes=[wslB[wi]])
                for tt in range(4):
                    t0 = tt * 512
                    ps, psB = kb.psb[1 + k % 4]
                    si = k % 3
                    k += 1
                    for c in range(NCH):
                        kb.mm(ps[:], wsl[wi][:, c * 128:(c + 1) * 128], xnT[:, c, t0:t0 + 512], c == 0, c == NCH - 1,
                              [wslB[wi], xnTB[tt]], [psB], inc=(c == NCH - 1))
                    kb.cp(stg[si], ps[:], [psB], [stgB[si]], eng=(ACT if k % 2 else DVE))
                    P.dma(SP, pT[oc, :, t0:t0 + 512], stg[si], reads=[stgB[si]], writes=[pTB[oc]])

        def mixer_attn(l):
            cm.off = MB0
            qT = cm.bf16([3, T]); qTBs = [Buf("qT%d" % i) for i in range(3)]
            kT = cm.bf16([T]); kTB = Buf("kT")
            wv = cm.bf16([D]); wvB = Buf("wv")
            vaug = cm.bf16([16, 2 * 66]); vaugB = Buf("vaug")
            sT = [cm.f32([384]) for i in range(4)]; sTB = [Buf("sT%d" % i) for i in range(4)]
            pE = [cm.bf16([3, 128]) for i in range(4)]; pEB = [Buf("pE%d" % i) for i in range(4)]
            den = cm.f32([6]); denB = Buf("den")
            yat = cm.f32([6, 64]); yatB = Buf("yat")
            for g in range(3):
                P.dma(POOL, qT[:, g, :], pT[g], reads=[pTB[g]], writes=[qTBs[g]])
            P.dma(POOL, kT, pT[3], reads=[pTB[3]], writes=[kTB])
            P.dma(SP, wv, win_b[l, 4 * 128:5 * 128, :], reads=[convB], writes=[wvB])
            kb.memset(vaug, 1.0, [vaugB])
            for n in range(16):
                ps, psB = kb.psb[1 + n % 2]
                for c in range(NCH):
                    kb.mm(ps[:, 0:128], xnT[:, c, n * 128:(n + 1) * 128], wv[:, c * 128:(c + 1) * 128], c == 0, c == NCH - 1,
                          [xnTB[n // 4], wvB], [psB], inc=(c == NCH - 1))
                vdst = vaug[:, n, :].rearrange("p (k d) -> p k d", d=66)[:, :, 0:64]
                kb.cp(vdst, ps[:, 0:128].rearrange("p (k d) -> p k d", d=64), [psB], [vaugB], eng=ACT)
            for n in range(16):
                kbs = [0, 1] if n > 0 else [1]
                for kvh in range(2):
                    r0 = kvh * 64
                    for kbi in kbs:
                        idx = kvh * 2 + kbi
                        ps, psB = kb.psb[1 + idx]
                        kblk = n - 1 + kbi
                        kb.mm(ps[:, 0:384], kT[r0:r0 + 64, kblk * 128:(kblk + 1) * 128],
                              qT[r0:r0 + 64, :, n * 128:(n + 1) * 128], True, True, [kTB] + qTBs, [psB])
                        kb.stt(sT[idx], ps[:, 0:384], 0.125,
                               biasT[:, kbi, kvh * 3:(kvh + 1) * 3, :].rearrange("p a b -> p (a b)"),
                               ALU.mult, ALU.add, [psB, biasB], [sTB[idx]])
                        kb.act(pE[idx], sT[idx].rearrange("p (a b) -> p a b", b=128), AF.Exp, [sTB[idx]], [pEB[idx]])
                po, poB = kb.psb[5 + n % 2]
                pov = po[:, 0:6 * 66].rearrange("p (a b) -> p a b", b=66)
                for kvh in range(2):
                    for g in range(3):
                        hh = kvh * 3 + g
                        for j, kbi in enumerate(kbs):
                            idx = kvh * 2 + kbi
                            kblk = n - 1 + kbi
                            kb.mm(pov[:, hh, 0:65], pE[idx][:, g, :], vaug[:, kblk, kvh * 66:kvh * 66 + 65],
                                  j == 0, j == len(kbs) - 1, [pEB[idx], vaugB], [poB],
                                  inc=(hh == 5 and j == len(kbs) - 1))
                kb.tt(den, pov[:, :, 64], esink[:, l, :], ALU.add, [poB, esinkB], [denB])
                kb.P.op(DVE, lambda e: e.reciprocal(out=den, in_=den), [denB], [denB])
                kb.tt(yat, pov[:, :, 0:64], den.unsqueeze(2).to_broadcast([128, 6, 64]), ALU.mult, [poB, denB], [yatB])
                pt, ptB = kb.psb[7]
                for j in range(3):
                    kb.tr(pt[:, j * 128:(j + 1) * 128], yat[:, 2 * j:2 * j + 2, :].rearrange("p a b -> p (a b)"), ident[:],
                          [yatB, identB], [ptB], inc=(j == 2))
                kb.cp(ycat[:, 0:3, n * 128:(n + 1) * 128], pt[:, 0:384].rearrange("p (a b) -> p a b", b=128),
                      [ptB], [ycatB[0], ycatB[1], ycatB[2]], eng=ACT)

        def mixer_rwkv(l):
            import os
            RWCUT = float(os.environ.get("RWCUT", "9"))
            cm.off = MB1
            ST = [[cm.f32([64]) for i in range(2)] for jj in range(3)]
            STB = [[Buf("ST%d%d" % (jj, i)) for i in range(2)] for jj in range(3)]
            scur = [0, 0, 0]
            o_pcpp = cm.off
            pcpp = cm.f32([22, 256]); pcB = Buf("pc"); ppB = Buf("pp")
            pc = pcpp[:, 0:10, :]; pp = pcpp[:, 11:21, :]
            lgw = pcpp[:, 0:3, :]; asig = pcpp[:, 3:6, :]; g_ = pcpp[:, 6:9, :]
            kk = pcpp[:, 11:14, :]; kmod = pcpp[:, 14:17, :]; a_ = pcpp[:, 17:20, :]
            o_psh = cm.off
            psh = cm.f32([10, 256]); pshB = Buf("psh")
            e1 = cm.f32([3, 256]); e1B = Buf("e1")
            o_e2 = cm.off
            e2 = cm.f32([3, 256]); e2B = Buf("e2")
            e2b = cm.f32([3, 256]); e2bB = Buf("e2b")
            b_ = cm.f32([3, 256]); bonus = cm.f32([3, 256])
            o_lP = cm.off
            lP = cm.f32([3, 256]); eP = cm.f32([3, 256])
            AR = cm.f32([3, 2 * 2 * 128]); bT = cm.f32([3, 256]); kT_ = cm.f32([3, 256])
            bc = cm.f32([3, 256]); kc = cm.f32([3, 256])
            yfm = e1
            GB = Buf("rwkv_pre")
            o_core = cm.off
            cm.f32([4300])
            ext_lists = [
                [[o_core, 4300]],
                [[o_pcpp + 9 * 256, 13 * 256], [o_lP, 768], [o_psh + 9 * 256, 256]],
                [[o_pcpp, 1536], [o_psh, 1536], [o_e2, 2304]],
            ]

            def ext_alloc(exts, shape):
                n = int(np.prod(shape))
                for e_ in exts:
                    if e_[1] >= n:
                        ap = arena[:, e_[0]:e_[0] + n]
                        e_[0] += n
                        e_[1] -= n
                        return Carver._shape(ap, shape)
                raise AssertionError("extent alloc failed %s" % (shape,))
            INST = []
            o_free = cm.off
            free_ext = [[o_free, AW - o_free]]
            YtA = ext_alloc(free_ext, [2, 3 * 128]); YtAB = Buf("YtA")
            for k_ in range(3):
                ex = ext_lists[k_]
                d = {}
                for nm, shp in (("TM", [4, 128]), ("AbT", [2, 256]), ("AkT", [2, 256]), ("NN0", [4, 128]), ("NN1", [4, 128]),
                                ("Nm", [2, 128]), ("R0", [2, 128]), ("R1", [2, 128]), ("RhT", [2, 128]), ("Phi", [128]),
                                ("Ytok", [2, 64]), ("cen", [2, 64]), ("sqv", [2, 64]), ("Z", [64]), ("yn", [2, 64]),
                                ("st2", [2]), ("st3", [2])):
                    d[nm] = ext_alloc(ex, shp)
                    d[nm + "B"] = Buf(nm + str(k_))
                INST.append(d)
            for jj in range(3):
                kb.memset(ST[jj][0], 0.0, [STB[jj][0]])
            def bcp(col0, n):
                return rwp[:, l, col0:col0 + n].unsqueeze(2).to_broadcast([128, n, 256])
            def bcd(col0, n):
                return rwd[:, l, col0:col0 + n].unsqueeze(2).to_broadcast([128, n, 256])
            RP = [GB, rwpB, rwdB, cstB]
            for tile in range(8):
                t0 = tile * 256
                P.barrier()
                src = pT[5:15, :, t0:t0 + 256].rearrange("c p t -> p c t")
                P.dma(SP, pc, src, reads=[pTB[5]], writes=[pcB])
                if tile == 0:
                    kb.memset(pp[:, :, 0:1], 0.0, [ppB])
                    P.dma(SP, pp[:, :, 1:256], pT[5:15, :, 0:255].rearrange("c p t -> p c t"), reads=[pTB[5]], writes=[ppB])
                else:
                    P.dma(SP, pp, pT[5:15, :, t0 - 1:t0 + 255].rearrange("c p t -> p c t"), reads=[pTB[5]], writes=[ppB])
                kb.tt(pp, pp, bcp(0, 10), ALU.mult, RP + [ppB], [GB, ppB])
                kb.tt(pc, pc, bcd(0, 10), ALU.mult, RP + [pcB], [GB, pcB])
                kb.tt(psh, pc, pp, ALU.add, RP, [GB])
                rr = psh[:, 0:3, :]; kraw = psh[:, 3:6, :]; vv = psh[:, 6:9, :]
                tw = e2[:, 0, :]; sgg = e2[:, 1, :]
                kb.act(tw[0:32, :], psh[0:32, 9, :], AF.Tanh, RP, [GB])
                kb.act(sgg[64:128, :], psh[64:128, 9, :], AF.Sigmoid, RP, [GB])
                for jj in range(3):
                    cs_ = slice(jj * 128, (jj + 1) * 128)
                    pw, pwB = kb.psb[1]; pa, paB = kb.psb[2]; pg, pgB = kb.psb[3]
                    kb.mm(pw[:, 0:256], lw[0:32, l, cs_], tw[0:32, :], True, True, RP + [lwB], [pwB])
                    kb.act(e1[:, jj, :], pw[:, 0:256], AF.Exp, [pwB] + RP, [GB], scale=-1.0, bias=rwd[:, l, 10 + jj:11 + jj])
                    kb.mm(pa[:, 0:256], lw[32:64, l, cs_], psh[32:64, 9, :], True, True, RP + [lwB], [paB])
                    kb.act(asig[:, jj, :], pa[:, 0:256], AF.Sigmoid, [paB] + RP, [GB], bias=rwp[:, l, 13 + jj:14 + jj])
                    kb.mm(pg[:, 0:256], lw[64:128, l, cs_], sgg[64:128, :], True, True, RP + [lwB], [pgB])
                    kb.cp(g_[:, jj, :], pg[:, 0:256], [pgB] + RP, [GB])
                kb.act(e1, e1, AF.Ln, RP, [GB], bias=cst[:, 0:1])
                kb.act(e1, e1, AF.Exp, RP, [GB], scale=-1.0, bias=cst[:, 1:2])
                kb.ts(lgw, e1, -1.0, None, ALU.mult, None, RP, [GB])
                kb.tt(kk, kraw, bcp(16, 3), ALU.mult, RP, [GB])
                kb.tt(e2, kk, kk, ALU.mult, RP, [GB])
                for jj in range(3):
                    pn_, pnB_ = kb.psb[4 + jj % 2]
                    kb.mm(pn_[:, 0:256], BD[:], e2[:, jj, :], True, True, RP + [BDB], [pnB_])
                    kb.act(e2b[:, jj, :], pn_[:, 0:256], AF.Sqrt, [pnB_] + RP, [GB], bias=cst[:, 2:3])
                kb.P.op(DVE, lambda e: e.reciprocal(out=e2b, in_=e2b), RP, [GB])
                kb.tt(kk, kk, e2b, ALU.mult, RP, [GB])
                kb.tt(kmod, asig, bcp(19, 3), ALU.mult, RP, [GB])
                kb.tt(kmod, kmod, bcd(13, 3), ALU.add, RP, [GB])
                kb.tt(kmod, kmod, kraw, ALU.mult, RP, [GB])
                kb.ts(a_, kk, -1.0, None, ALU.mult, None, RP, [GB])
                kb.tt(b_, kk, asig, ALU.mult, RP, [GB])
                kb.tt(e2, rr, kmod, ALU.mult, RP, [GB])
                kb.tt(e2, e2, bcp(22, 3), ALU.mult, RP, [GB])
                for jj in range(3):
                    pb_, pbB_ = kb.psb[6 + jj % 2]
                    kb.mm(pb_[:, 0:256], BD[:], e2[:, jj, :], True, True, RP + [BDB], [pbB_])
                    kb.tt(bonus[:, jj, :], pb_[:, 0:256], vv[:, jj, :], ALU.mult, [pbB_] + RP, [GB])
                for jj in range(3):
                    for c in range(2):
                        cs = slice(c * 128, (c + 1) * 128)
                        kb.P.op(DVE, lambda e, o=lP[:, jj, cs], d1=lgw[:, jj, cs]: e.tensor_tensor_scan(
                            out=o, data0=ones128[:], data1=d1, initial=0.0, op0=ALU.mult, op1=ALU.add), RP + [ones128B], [GB])
                kb.act(eP, lP, AF.Exp, RP, [GB])
                kb.tt(e2, lP, lgw, ALU.subtract, RP, [GB])
                kb.act(e2, e2, AF.Exp, RP, [GB])
                kb.act(e2b, lP, AF.Exp, RP, [GB], scale=-1.0)
                AR5 = AR.rearrange("p j (c a t) -> p j c a t", c=2, a=2)
                for jj in range(3):
                    kb.tt(AR5[:, jj, :, 0, :], a_[:, jj, :].rearrange("p (c t) -> p c t", c=2),
                          e2[:, jj, :].rearrange("p (c t) -> p c t", c=2), ALU.mult, RP, [GB])
                    kb.tt(AR5[:, jj, :, 1, :], rr[:, jj, :].rearrange("p (c t) -> p c t", c=2),
                          eP[:, jj, :].rearrange("p (c t) -> p c t", c=2), ALU.mult, RP, [GB])
                kb.tt(bT, b_, e2b, ALU.mult, RP, [GB])
                kb.tt(kT_, kmod, e2b, ALU.mult, RP, [GB])
                for jj in range(3):
                    for c in range(2):
                        cs = slice(c * 128, (c + 1) * 128)
                        pcol = eP[:, jj, c * 128 + 127:c * 128 + 128]
                        kb.ts(bc[:, jj, cs], bT[:, jj, cs], pcol, None, ALU.mult, None, RP, [GB])
                        kb.ts(kc[:, jj, cs], kT_[:, jj, cs], pcol, None, ALU.mult, None, RP, [GB])
                P.barrier()

                def core(jj, c):
                    I = INST[jj]
                    TM, AbT, AkT, Nm, RhT, Phi, Zz, Ytok = I["TM"], I["AbT"], I["AkT"], I["Nm"], I["RhT"], I["Phi"], I["Z"], I["Ytok"]
                    TMB, AbTB, AkTB, NmB, RhTB, PhiB, ZB, YtokB = (I[n_ + "B"] for n_ in ("TM", "AbT", "AkT", "Nm", "RhT", "Phi", "Z", "Ytok"))
                    NN = [I["NN0"], I["NN1"]]; NNB = [I["NN0B"], I["NN1B"]]
                    Rr = [I["R0"], I["R1"]]; RB = [I["R0B"], I["R1B"]]
                    cen, sqv, yn, st2, st3 = I["cen"], I["sqv"], I["yn"], I["st2"], I["st3"]
                    gnB = I["cenB"]
                    bk0 = int(os.environ.get("RWBANK", "4")) if jj == 2 else 2 * jj
                    bks = [kb.psb[bk0], kb.psb[bk0 + 1]]
                    ctr = [0]

                    def nb():
                        r_ = bks[ctr[0] % 2]
                        ctr[0] += 1
                        return r_
                    cs = slice(c * 128, (c + 1) * 128)
                    aT = AR5[:, jj, c, 0, :]
                    arT = AR5[:, jj, c, :, :].rearrange("p a t -> p (a t)")
                    pt, ptB = nb()
                    for i, srcT in enumerate((aT, vv[:, jj, cs], bc[:, jj, cs], kc[:, jj, cs])):
                        kb.tr(pt[:, i * 128:(i + 1) * 128], srcT, ident[:], [GB, identB], [ptB], inc=(i == 3))
                    kb.cp(TM.rearrange("p a b -> p (a b)"), pt[:], [ptB], [TMB], eng=ACT)
                    At = TM[:, 0, :]; Vt = TM[:, 1, :]; Bc = TM[:, 2, :]; Kc = TM[:, 3, :]
                    yield
                    if RWCUT <= 1:
                        return
                    mUf = mU[:].rearrange("p a t -> p (a t)")
                    pgs = [nb(), nb()]
                    for hh in range(2):
                        rows = slice(hh * 64, hh * 64 + 64)
                        pg, pgB = pgs[hh]
                        kb.mm(pg[:, 0:256], bT[rows, jj, cs], arT[rows, :], True, True, [GB], [pgB], inc=False)
                        kb.mm(pg[:, 256:512], kT_[rows, jj, cs], arT[rows, :], True, True, [GB], [pgB])
                    for hh in range(2):
                        pg, pgB = pgs[hh]
                        kb.tt(AbT[:, hh, :], pg[:, 0:256], mUf, ALU.mult, [pgB, mUB], [AbTB])
                        kb.tt(AkT[:, hh, :], pg[:, 256:512], mUf, ALU.mult, [pgB, mUB], [AkTB], eng=DVE)
                    AbT4 = AbT.rearrange("p h (a t) -> p h a t", a=2)
                    AkT4 = AkT.rearrange("p h (a t) -> p h a t", a=2)
                    yield
                    if RWCUT <= 2:
                        return
                    pm, pmB = nb()
                    for hh in range(2):
                        kb.tr(pm[:, hh * 128:(hh + 1) * 128], AbT4[:, hh, 0, :], ident[:], [AbTB, identB], [pmB], inc=False)
                    for hh in range(2):
                        kb.mm(pm[:, 256 + hh * 64:256 + (hh + 1) * 64], AkT4[:, hh, 0, :], Vt[:, hh * 64:(hh + 1) * 64], True, True,
                              [AkTB, TMB], [pmB], inc=(hh == 1))
                    kb.cp(Nm.rearrange("p h x -> p (h x)"), pm[:, 0:256], [pmB], [NmB], eng=ACT)
                    kb.cp(Rr[0][:, 1, :], pm[:, 256:384], [pmB], [RB[0]], eng=ACT)
                    kb.cp(Rr[0][:, 0, :], At, [TMB], [RB[0]], eng=POOL)
                    yield
                    if RWCUT <= 3:
                        return
                    for i in range(7):
                        if i == 0:
                            NTi = lambda hh: AbT4[:, hh, 0, :]
                            Ni = lambda hh: Nm[:, hh, :]
                            nB = [AbTB, NmB]
                        else:
                            nn = NN[i % 2]
                            NTi = lambda hh, nn=nn: nn[:, hh, :]
                            Ni = lambda hh, nn=nn: nn[:, 2 + hh, :]
                            nB = [NNB[i % 2]]
                        pr, prB = nb()
                        for hh in range(2):
                            kb.mm(pr[:, hh * 128:(hh + 1) * 128],
                                  NTi(hh), Rr[i % 2].rearrange("p a (h k) -> p a h k", h=2)[:, :, hh, :], True, True,
                                  nB + [RB[i % 2]], [prB], inc=(hh == 1))
                        if i < 6:
                            pn, pnB = nb()
                            for hh in range(2):
                                kb.mm(pn[:, hh * 128:(hh + 1) * 128], Ni(hh), NTi(hh), True, True, nB, [pnB], inc=False)
                                kb.mm(pn[:, 256 + hh * 128:256 + (hh + 1) * 128], NTi(hh), Ni(hh), True, True, nB, [pnB], inc=(hh == 1))
                        for hh in range(2):
                            kb.tt(Rr[(i + 1) % 2].rearrange("p a (h k) -> p a h k", h=2)[:, :, hh, :],
                                  pr[:, hh * 128:(hh + 1) * 128].rearrange("p (a k) -> p a k", a=2),
                                  Rr[i % 2].rearrange("p a (h k) -> p a h k", h=2)[:, :, hh, :],
                                  ALU.add, [prB, RB[i % 2]], [RB[(i + 1) % 2]])
                        if i < 6:
                            kb.cp(NN[(i + 1) % 2].rearrange("p a b -> p (a b)"), pn[:], [pnB], [NNB[(i + 1) % 2]], eng=ACT)
                        yield
                    if RWCUT <= 4:
                        return
                    Rf = Rr[1]; RfB = RB[1]
                    Ahat = Rf[:, 0, :]
                    W0 = Rf[:, 1, :]
                    pq, pqB = nb()
                    rt5 = AR5[:, jj, c, 1, :]
                    for hp in range(2):
                        kb.mm(pq[:, hp * 128:(hp + 1) * 128], Ahat, AbT4[:, hp, 1, :], True, True, [RfB, AbTB], [pqB], inc=False)
                    kb.mm(pq[:, 256:384], Ahat, Bc, True, True, [RfB, TMB], [pqB], inc=False)
                    kb.mm(pq[:, 384:512], Bc, W0, True, False, [TMB, RfB], [pqB], inc=False)
                    kb.mm(pq[:, 384:512], Kc, Vt, False, True, [TMB], [pqB])
                    for hp in range(2):
                        kb.tt(RhT[:, hp, :], pq[:, hp * 128:(hp + 1) * 128], rt5, ALU.add, [pqB, GB], [RhTB])
                        kb.ts(RhT[:, hp, :], RhT[:, hp, :], rowm[:, hp:hp + 1], None, ALU.mult, None, [RhTB, rowmB], [RhTB])
                    kb.tt(Phi, pq[:, 256:384], BD[:], ALU.mult, [pqB, BDB], [PhiB])
                    kb.stt(Phi, ident[:], eP[:, jj, c * 128 + 127:c * 128 + 128], Phi, ALU.mult, ALU.add, [identB, GB, PhiB], [PhiB])
                    kb.ts(Zz, pq[:, 384:448], rowm[:, 0:1], None, ALU.mult, None, [pqB, rowmB], [ZB])
                    kb.stt(Zz, pq[:, 448:512], rowm[:, 1:2], Zz, ALU.mult, ALU.add, [pqB, rowmB, ZB], [ZB])
                    yield
                    if RWCUT <= 5:
                        return
                    cur = scur[jj]
                    py, pyB = nb()
                    for hp in range(2):
                        osl = py[:, hp * 64:(hp + 1) * 64]
                        kb.mm(osl, AbT4[:, hp, 1, :], Rf[:, 1, hp * 64:(hp + 1) * 64], True, False, [AbTB, RfB], [pyB], inc=False)
                        kb.mm(osl, AkT4[:, hp, 1, :], Vt[:, hp * 64:(hp + 1) * 64], False, False, [AkTB, TMB], [pyB], inc=False)
                        kb.mm(osl, RhT[:, hp, :], ST[jj][cur], False, True, [RhTB, STB[jj][cur]], [pyB], inc=False)
                    kb.mm(py[:, 128:192], Phi, ST[jj][cur], True, True, [PhiB, STB[jj][cur]], [pyB])
                    kb.cp(YtA[:, c, jj * 128:(jj + 1) * 128], py[:, 0:128], [pyB], [YtAB], eng=ACT)
                    kb.tt(ST[jj][1 - cur], py[:, 128:192], Zz, ALU.add, [pyB, ZB], [STB[jj][1 - cur]])
                    scur[jj] = 1 - cur
                    yield
                    if not os.environ.get("RWOLDGN"):
                        return
                    yield "tail"
                    if RWCUT <= 6:
                        return
                    if os.environ.get("RWINST") and str(jj) not in os.environ["RWINST"]:
                        return
                    G = [YtokB, gnB, cstB]
                    kb.P.op(DVE, lambda e: e.reduce_sum(out=st2, in_=Ytok, axis=AX.X), G, [gnB])
                    kb.ts(st2, st2, -1.0 / 64, None, ALU.mult, None, G, [gnB])
                    kb.tt(cen, Ytok, st2.unsqueeze(2).to_broadcast([128, 2, 64]), ALU.add, G, [gnB])
                    kb.tt(sqv, cen, cen, ALU.mult, G, [gnB])
                    kb.P.op(DVE, lambda e: e.reduce_sum(out=st3, in_=sqv, axis=AX.X), G, [gnB])
                    kb.act(st3, st3, AF.Ln, G, [gnB], scale=1.0 / 64, bias=cst[:, 3:4])
                    if RWCUT <= 6.5:
                        return
                    kb.act(st3, st3, AF.Exp, G, [gnB], scale=-0.5)
                    if RWCUT <= 6.6:
                        return
                    kb.tt(yn, cen, st3.unsqueeze(2).to_broadcast([128, 2, 64]), ALU.mult, G, [gnB])
                    if RWCUT <= 6.7:
                        return
                    pt2, pt2B = nb()
                    kb.tr(pt2[:, 0:128], yn.rearrange("p h v -> p (h v)"), ident[:], [gnB, identB], [pt2B])
                    kb.ts(yfm[:, jj, cs], pt2[:, 0:128], rwp[:, l, 25 + jj:26 + jj], rwp[:, l, 28 + jj:29 + jj],
                          ALU.mult, ALU.add, [pt2B, rwpB, GB], [I["ynB"]])
                    yield

                for c in range(2):
                    gens = [core(jj, c) for jj in range(2 if os.environ.get("RW2") else 3)]
                    live = list(gens)
                    if os.environ.get("RWSEQ"):
                        for g_i in gens:
                            for _ in g_i:
                                pass
                        live = []
                    tails = []
                    while live:
                        for g_i in list(live):
                            try:
                                if next(g_i) == "tail":
                                    live.remove(g_i)
                                    tails.append(g_i)
                            except StopIteration:
                                live.remove(g_i)
                    if tails:
                        P.barrier()
                    for g_i in tails:
                        for _ in g_i:
                            pass
                    if os.environ.get("RW2") and not os.environ.get("RWSEQ"):
                        for _ in core(2, c):
                            pass
                P.barrier()
                if RWCUT <= 7:
                    continue
                if not os.environ.get("RWOLDGN"):
                    cenA = arena[:, o_core:o_core + 768].rearrange("p (n v) -> p n v", v=64)
                    sqA = arena[:, o_core + 768:o_core + 1536].rearrange("p (n v) -> p n v", v=64)
                    stA = arena[:, o_core + 1536:o_core + 1548]
                    stB = arena[:, o_core + 1552:o_core + 1564]
                    gA = Buf("gnA")
                    Yv = YtA.rearrange("p c (n v) -> p (c n) v", v=64)
                    GA = [YtAB, gA, cstB]
                    kb.P.op(DVE, lambda e: e.reduce_sum(out=stA, in_=Yv, axis=AX.X), GA, [gA])
                    kb.ts(stA, stA, -1.0 / 64, None, ALU.mult, None, GA, [gA])
                    kb.tt(cenA, Yv, stA.unsqueeze(2).to_broadcast([128, 12, 64]), ALU.add, GA, [gA])
                    kb.tt(sqA, cenA, cenA, ALU.mult, GA, [gA])
                    kb.P.op(DVE, lambda e: e.reduce_sum(out=stB, in_=sqA, axis=AX.X), GA, [gA])
                    kb.act(stB, stB, AF.Ln, GA, [gA], scale=1.0 / 64, bias=cst[:, 3:4])
                    kb.act(stB, stB, AF.Exp, GA, [gA], scale=-0.5)
                    kb.tt(cenA, cenA, stB.unsqueeze(2).to_broadcast([128, 12, 64]), ALU.mult, GA, [gA])
                    cen2 = cenA.rearrange("p (c j h) v -> p c j (h v)", c=2, j=3)
                    for c in range(2):
                        ptg, ptgB = kb.psb[6 + c]
                        for jj in range(3):
                            kb.tr(ptg[:, jj * 128:(jj + 1) * 128], cen2[:, c, jj, :], ident[:], [gA, identB], [ptgB], inc=(jj == 2))
                        ysl = yfm[:, :, c * 128:(c + 1) * 128]
                        kb.tt(ysl, ptg[:, 0:384].rearrange("p (j t) -> p j t", j=3),
                              rwp[:, l, 25:28].unsqueeze(2).to_broadcast([128, 3, 128]), ALU.mult, [ptgB, rwpB, GB], [GB])
                        kb.tt(ysl, ysl, rwp[:, l, 28:31].unsqueeze(2).to_broadcast([128, 3, 128]), ALU.add, [rwpB, GB], [GB])
                kb.tt(yfm, yfm, bonus, ALU.add, RP, [GB])
                kb.tt(ycat[:, 3:6, t0:t0 + 256], yfm, g_, ALU.mult, RP, [ycatB[3], ycatB[4], ycatB[5]])

        def mixer_ssm(l):
            cm.off = MB1
            bbj = cm.f32([2, 128]); bbB = Buf("bbj")
            ccj = cm.f32([2, 128]); ccB = Buf("ccj")
            uT = cm.f32([2, T]); uTB = [Buf("uT0"), Buf("uT1")]
            yacc = cm.f32([2, T]); yaccB = [Buf("yacc0"), Buf("yacc1")]
            tab = cm.f32([4, 512]); tabB = Buf("tab")
            rhoj = cm.f32([512]); rhojB = Buf("rhoj")
            m = [[cm.f32([512]) for i in range(4)] for u_ in range(2)]
            mB = [[Buf("m%d%d" % (u_, i)) for i in range(4)] for u_ in range(2)]
            bpr = [cm.f32([512]) for u_ in range(2)]; bpi = [cm.f32([512]) for u_ in range(2)]
            bpB = [Buf("bp0"), Buf("bp1")]
            wre = [cm.f32([512]) for u_ in range(2)]; wim = [cm.f32([512]) for u_ in range(2)]
            wB_ = [Buf("w0"), Buf("w1")]
            xr = [cm.f32([512]) for i in range(2)]; xi = [cm.f32([512]) for i in range(2)]
            xB = [Buf("x0"), Buf("x1")]
            car = [cm.f32([4]) for u_ in range(2)]; carB = [Buf("car0"), Buf("car1")]
            zero1 = cm.f32([1]); zB = Buf("zero1")
            sgm = cm.f32([512]); sgmB = Buf("sgm")
            kb.memset(zero1, 0.0, [zB])
            for c in range(2):
                P.dma(SP, uT[:, c, :], pT[15 + c], reads=[pTB[15 + c]], writes=[uTB[c]])
                kb.ts(yacc[:, c, :], uT[:, c, :], s5v[:, l, 0, c:c + 1], None, ALU.mult, None, [uTB[c], s5vB], [yaccB[c]])
            Ec, Es, Tc, Tn = (tab[:, a, :] for a in range(4))

            def stA(u):
                j, tt = u // 4, u % 4
                c = j // 4
                t0 = tt * 512
                ub = u % 2
                if tt == 0:
                    P.dma(SP, tab, tabs[l, j], writes=[tabB])
                    P.dma(SP, bbj, bb_d[:, l, :, j, :], writes=[bbB])
                    P.dma(SP, ccj, cc_d[:, l, :, j, :], writes=[ccB])
                    kb.ts(ccj[:, 1, :], ccj[:, 1, :], -1.0, None, ALU.mult, None, [ccB], [ccB])
                    kb.cp(rhoj, rho[:, l, j:j + 1].to_broadcast([128, 512]), [rhoB], [rhojB])
                pR, pRB = kb.psb[1 + ub]
                pI, pIB = kb.psb[3 + ub]
                kb.mm(pR[:], bbj[:, 0, :], uT[:, c, t0:t0 + 512], True, True, [bbB, uTB[c]], [pRB])
                kb.mm(pI[:], bbj[:, 1, :], uT[:, c, t0:t0 + 512], True, True, [bbB, uTB[c]], [pIB])
                mm_, mmB = m[ub], mB[ub]
                kb.tt(mm_[0], pR[:], Tc, ALU.mult, [pRB, tabB], [mmB[0]])
                kb.tt(mm_[1], pI[:], Tn, ALU.mult, [pIB, tabB], [mmB[1]])
                kb.tt(mm_[2], pR[:], Tn, ALU.mult, [pRB, tabB], [mmB[2]])
                kb.tt(mm_[3], pI[:], Tc, ALU.mult, [pIB, tabB], [mmB[3]])
                kb.tt(bpr[ub], mm_[0], mm_[1], ALU.subtract, [mmB[0], mmB[1]], [bpB[ub]], eng=POOL)
                kb.tt(bpi[ub], mm_[2], mm_[3], ALU.add, [mmB[2], mmB[3]], [bpB[ub]], eng=POOL)

            def stB(u):
                tt = u % 4
                ub = u % 2
                if tt == 0:
                    inr, ini, inB = zero1, zero1, zB
                else:
                    inr, ini, inB = car[1 - ub][:, 0:1], car[1 - ub][:, 1:2], carB[1 - ub]
                kb.P.op(DVE, lambda e, o=wre[ub], d1=bpr[ub], i0=inr: e.tensor_tensor_scan(
                    out=o, data0=rhoj, data1=d1, initial=i0, op0=ALU.mult, op1=ALU.add), [rhojB, bpB[ub], inB], [wB_[ub]])
                kb.P.op(DVE, lambda e, o=wim[ub], d1=bpi[ub], i0=ini: e.tensor_tensor_scan(
                    out=o, data0=rhoj, data1=d1, initial=i0, op0=ALU.mult, op1=ALU.add), [rhojB, bpB[ub], inB], [wB_[ub]])
                if tt < 3:
                    wl_r, wl_i = wre[ub][:, 511:512], wim[ub][:, 511:512]
                    ec_l, es_l = Ec[:, 511:512], Es[:, 511:512]
                    cr = car[ub]
                    RC = [wB_[ub], tabB, carB[ub]]
                    kb.tt(cr[:, 2:3], wl_i, es_l, ALU.mult, RC, [carB[ub]])
                    kb.tt(cr[:, 3:4], wl_i, ec_l, ALU.mult, RC, [carB[ub]])
                    kb.stt(cr[:, 0:1], wl_r, ec_l, cr[:, 2:3], ALU.mult, ALU.subtract, RC, [carB[ub]])
                    kb.stt(cr[:, 1:2], wl_r, es_l, cr[:, 3:4], ALU.mult, ALU.add, RC, [carB[ub]])

            def stC(u):
                j, tt = u // 4, u % 4
                ub = u % 2
                mm_, mmB = m[ub], mB[ub]
                kb.tt(mm_[0], wre[ub], Ec, ALU.mult, [wB_[ub], tabB], [mmB[0]])
                kb.tt(mm_[1], wim[ub], Es, ALU.mult, [wB_[ub], tabB], [mmB[1]])
                kb.tt(mm_[2], wre[ub], Es, ALU.mult, [wB_[ub], tabB], [mmB[2]])
                kb.tt(mm_[3], wim[ub], Ec, ALU.mult, [wB_[ub], tabB], [mmB[3]])
                kb.tt(xr[ub], mm_[0], mm_[1], ALU.subtract, [mmB[0], mmB[1]], [xB[ub]], eng=POOL)
                kb.tt(xi[ub], mm_[2], mm_[3], ALU.add, [mmB[2], mmB[3]], [xB[ub]], eng=POOL)
                pY, pYB = kb.psb[5 + ub]
                kb.mm(pY[:], ccj[:, 0, :], xr[ub], True, False, [ccB, xB[ub]], [pYB], inc=False)
                kb.mm(pY[:], ccj[:, 1, :], xi[ub], False, True, [ccB, xB[ub]], [pYB])

            def stD(u):
                j, tt = u // 4, u % 4
                c = j // 4
                t0 = tt * 512
                pY, pYB = kb.psb[5 + u % 2]
                kb.tt(yacc[:, c, t0:t0 + 512], yacc[:, c, t0:t0 + 512], pY[:], ALU.add, [yaccB[c], pYB], [yaccB[c]])

            NU = 32
            stA(0)
            for u in range(NU):
                stB(u)
                stC(u)
                if u + 1 < NU and (u + 1) % 4 != 0:
                    stA(u + 1)
                if u >= 1:
                    stD(u - 1)
                if u + 1 < NU and (u + 1) % 4 == 0:
                    stA(u + 1)
            stD(NU - 1)
            k = 0
            for c in range(2):
                kb.act(yacc[:, c, :], yacc[:, c, :], AF.Gelu, [yaccB[c]], [yaccB[c]])
            for oc2 in range(2):
                for tt in range(4):
                    t0 = tt * 512
                    ps, psB = kb.psb[1 + k % 2]
                    k += 1
                    for c2 in range(2):
                        kb.mm(ps[:], gluw[:, l, oc2, c2, :], yacc[:, c2, t0:t0 + 512], c2 == 0, c2 == 1,
                              [gluwB, yaccB[c2]], [psB], inc=(c2 == 1))
                    kb.act(sgm, ps[:], AF.Sigmoid, [psB, s5vB], [sgmB], bias=s5v[:, l, 1, oc2:oc2 + 1])
                    kb.tt(ycat[:, 6 + oc2, t0:t0 + 512], yacc[:, oc2, t0:t0 + 512], sgm, ALU.mult,
                          [yaccB[oc2], sgmB], [ycatB[6 + oc2]])

        def mixer_outproj(l):
            cm.off = MB0
            msq = cm.bf16([NCH, 512]); msqB = Buf("msq")
            mr = cm.f32([512]); mrB = Buf("mr")
            mr0 = cm.f32([512]); mr0B = Buf("mr0")
            wsl = [cm.bf16([D]) for i in range(3)]; wslB = [Buf("wsl%d" % i) for i in range(3)]
            mo = cm.f32([NCH, 512]); moB = Buf("mo")
            tm = [cm.f32([512]) for i in range(2)]; tmB = [Buf("tm%d" % i) for i in range(2)]
            gpost = 6 * l + 3
            k = 0
            for tt in range(4):
                t0 = tt * 512
                for dc in range(NCH):
                    wi = k % 3
                    P.dma(SP, wsl[wi], wout_b[l, dc * 128:(dc + 1) * 128, :], reads=[convB], writes=[wslB[wi]])
                    ps, psB = kb.psb[1 + k % 4]
                    k += 1
                    for c in range(NCH):
                        kb.mm(ps[:], wsl[wi][:, c * 128:(c + 1) * 128], ycat[:, c, t0:t0 + 512], c == 0, c == NCH - 1,
                              [wslB[wi], ycatB[c]], [psB], inc=(c == NCH - 1))
                    kb.cp(mo[:, dc, :], ps[:], [psB], [moB], eng=ACT)
                    kb.act(msq[:, dc, :], ps[:], AF.Square, [psB], [msqB])
                rms_rstd(lambda c: msq[:, c, :], msqB, mr, mrB, mr0, mr0B)
                for c in range(NCH):
                    ti = c % 2
                    kb.stt(tm[ti], mo[:, c, :], g32[:, gpost, c:c + 1], mr, ALU.mult, ALU.mult,
                           [moB, g32B, mrB], [tmB[ti]])
                    kb.tt(h[:, c, t0:t0 + 512], h[:, c, t0:t0 + 512], tm[ti], ALU.add, [hB[tt], tmB[ti]], [hB[tt]], eng=POOL)

        def mixer(l, s):
            P.barrier()
            if "inproj" in parts:
                mixer_inproj(l)
                P.barrier()
            if "attn" in parts:
                mixer_attn(l)
                P.barrier()
            if "attn" not in parts:
                kb.memset(ycat[:, 0:3, :], 0.0, [ycatB[0], ycatB[1], ycatB[2]])
            if "ssm" not in parts:
                kb.memset(ycat[:, 6:8, :], 0.0, [ycatB[6], ycatB[7]])
            import os as _os
            kb.memset(ycat[:, 3:6, :], float("nan") if _os.environ.get("RWNAN") else 0.0, [ycatB[3], ycatB[4], ycatB[5]])
            if "ssm" in parts:
                mixer_ssm(l)
                P.barrier()
            if "rwkv" in parts:
                mixer_rwkv(l)
                P.barrier()
            if debug == "ycat%d" % l:
                cm.off = MB0
                dstg = [cm.f32([T]) for i in range(2)]
                dB = [Buf("d0"), Buf("d1")]
                for c in range(NCH):
                    kb.cp(dstg[c % 2], ycat[:, c, :], [ycatB[c]], [dB[c % 2]])
                    P.dma(SP, dbg[s, c], dstg[c % 2], reads=[dB[c % 2]], final=True)
                P.barrier()
            if "outproj" in parts:
                mixer_outproj(l)
                P.barrier()

        for s in range(nseq):
            for c in range(NCH):
                P.dma(SP, h[:, c, :], xT[s, c], writes=hB)
            P.barrier()
            if s == 0:
                for o_, i_ in conv_rest:
                    P.dma(POOL, o_, i_, writes=[wB])
            phase = 0
            for l in range(n_layers):
                if phase < stop_after and "ffn" in parts:
                    ffn(2 * l, 6 * l + 0, 6 * l + 1)
                phase += 1
                if phase < stop_after:
                    mixer(l, s)
                phase += 1
                if phase < stop_after and "ffn" in parts:
                    ffn(2 * l + 1, 6 * l + 4, 6 * l + 5)
                phase += 1
            for c in range(NCH):
                P.dma(SP, outT[s, c], h[:, c, :], reads=hB, final=True)
        P.emit()
    return nc


def _buckets():
    W = 128
    qi = np.arange(W)[:, None]
    kj = np.arange(2 * W)[None, :]
    rel = qi + W - kj
    inw = (rel >= 0) & (rel < W)
    n = np.maximum(rel, 0)
    nf = np.maximum(n, 1).astype(np.float32)
    large = 16 + (np.log(nf / 16) / np.float32(np.log(128 / 16)) * 16).astype(np.int32)
    large = np.minimum(large, 31)
    return np.where(n < 16, n, large), inw


def host_layout(inputs):
    L = 2
    wgu = np.empty((4, 2, DFF, D), np.float32)
    wdn = np.empty((4, D, DFF), np.float32)
    ffw = {"ffn1": (inputs["ffn1_w_gate"], inputs["ffn1_w_up"], inputs["ffn1_w_down"]),
           "ffn2": (inputs["ffn2_w_gate"], inputs["ffn2_w_up"], inputs["ffn2_w_down"])}
    for l in range(L):
        for j, nm in enumerate(("ffn1", "ffn2")):
            fi = 2 * l + j
            for m in range(2):
                W = ffw[nm][m][l]
                wgu[fi, m] = W.reshape(NCH, 128, NF, 128).transpose(2, 1, 0, 3).reshape(DFF, D)
            Wd = ffw[nm][2][l]
            wdn[fi] = Wd.reshape(NF, 128, NCH, 128).transpose(2, 1, 0, 3).reshape(D, DFF)
    gl = []
    for l in range(L):
        for nm in ("ln_pre_ffn1", "ln_post_ffn1", "ln_pre_mix", "ln_post_mix", "ln_pre_ffn2", "ln_post_ffn2"):
            gl.append(inputs[nm][l].reshape(NCH, 128).T)
    gains = np.ascontiguousarray(np.stack(gl, axis=1)).astype(np.float32)
    qperm = []
    for g in range(3):
        qperm += list(range(g * 64, (g + 1) * 64)) + list(range((3 + g) * 64, (4 + g) * 64))
    cols = np.array(qperm + list(range(384, 2176)))
    win = np.empty((L, NOC * 128, D), np.float32)
    wout = np.empty((L, D, D), np.float32)
    for l in range(L):
        Wp = inputs["w_in"][l][:, cols]
        win[l] = Wp.reshape(NCH, 128, NOC, 128).transpose(2, 1, 0, 3).reshape(NOC * 128, D)
        wout[l] = inputs["w_out"][l].reshape(NCH, 128, NCH, 128).transpose(2, 1, 0, 3).reshape(D, D)
    bucket, inw = _buckets()
    rb = inputs["rel_bias"]
    bt = rb[bucket]
    biasT = np.ascontiguousarray(bt.reshape(128, 2, 128, 6).transpose(2, 1, 3, 0)).astype(np.float32)
    maskT = np.where(inw.reshape(128, 2, 128).transpose(2, 1, 0), 0.0, -30000.0).astype(np.float32)
    sinks = np.ascontiguousarray(np.broadcast_to(inputs["att_sinks"][None], (128, L, 6))).astype(np.float32)
    s5p = np.zeros((128, L, 3, 8), np.float32)
    bb = np.zeros((128, L, 2, 8, 128), np.float32)
    cc = np.zeros((128, L, 2, 8, 128), np.float32)
    s5v = np.zeros((128, L, 2, 2), np.float32)
    gluw = np.zeros((128, L, 2, 2, 128), np.float32)
    for l in range(L):
        for j in range(8):
            for g2 in range(2):
                g = 2 * j + g2
                ps_ = slice(g2 * 64, (g2 + 1) * 64)
                s5p[ps_, l, 0, j] = inputs["ssm_lambda_re"][l, g]
                s5p[ps_, l, 1, j] = inputs["ssm_lambda_im"][l, g]
                s5p[ps_, l, 2, j] = inputs["ssm_log_dt"][l, g]
                q0 = (g % 8) * 16
                bb[q0:q0 + 16, l, 0, j, ps_] = inputs["ssm_b_re"][l, g].T
                bb[q0:q0 + 16, l, 1, j, ps_] = inputs["ssm_b_im"][l, g].T
                cc[ps_, l, 0, j, q0:q0 + 16] = inputs["ssm_c_re"][l, g].T
                cc[ps_, l, 1, j, q0:q0 + 16] = inputs["ssm_c_im"][l, g].T
        s5v[:, l, 0, :] = inputs["ssm_d"][l].reshape(2, 128).T
        s5v[:, l, 1, :] = inputs["ssm_glu_b"][l].reshape(2, 128).T
        gluw[:, l] = inputs["ssm_glu_w"][l].reshape(2, 128, 2, 128).transpose(1, 2, 0, 3)
    rwp = np.zeros((128, L, 31), np.float32)
    lw = np.zeros((128, L, 384), np.float32)
    for l in range(L):
        rwp[:, l, 0:10] = inputs["rwkv_mu"][l].reshape(10, 128).T
        for i, nm in enumerate(("rwkv_w0", "rwkv_a0", "rwkv_k_k", "rwkv_k_a", "rwkv_r_k", "rwkv_gn_w", "rwkv_gn_b")):
            rwp[:, l, 10 + 3 * i:13 + 3 * i] = inputs[nm][l].reshape(3, 128).T
        lw[0:32, l] = inputs["rwkv_w2"][l]
        lw[32:64, l] = inputs["rwkv_a2"][l]
        lw[64:128, l] = inputs["rwkv_g2"][l]
    ii = np.arange(128)
    mU = np.stack([(ii[:, None] < ii[None, :]), (ii[:, None] <= ii[None, :])], axis=1).astype(np.float32)
    mL = (ii[None, :] < ii[:, None]).astype(np.float32)
    BDm = (ii[:, None] // 64 == ii[None, :] // 64).astype(np.float32)
    rowm = np.stack([(ii < 64), (ii >= 64)], axis=1).astype(np.float32)
    return {"rwp": rwp, "lw": lw, "mU": mU, "mL": mL, "BD": BDm, "rowm": rowm, "s5p": s5p, "bb": bb, "cc": cc, "s5v": s5v, "gluw": gluw,
            "wgu": wgu, "wdn": wdn, "gains": gains, "win": win, "wout": wout,
            "ident": np.eye(128, dtype=np.float32), "biasT": biasT, "maskT": maskT, "sinks": sinks}


def run(inputs, nseq, ncores, stop_after=99, debug=None, parts=ALL_PARTS):
    shared = host_layout(inputs)
    x = inputs["x"]
    nc = build(nseq, stop_after, debug=debug, parts=parts)
    in_maps = []
    for c in range(ncores):
        xs = x[c * nseq:(c + 1) * nseq]
        xt = np.ascontiguousarray(xs.transpose(0, 2, 1)).reshape(nseq, NCH, 128, T)
        m = dict(shared)
        m["xT"] = xt
        in_maps.append(m)
    res = run_bass_kernel_spmd(nc, in_maps, core_ids=list(range(ncores)))
    outs = []
    dbgs = []
    for c in range(ncores):
        outs.append(res.results[c]["outT"].reshape(nseq, D, T).transpose(0, 2, 1))
        if debug is not None:
            dbgs.append(res.results[c]["dbg"].reshape(nseq, D, T).transpose(0, 2, 1))
    out = np.ascontiguousarray(np.concatenate(outs, axis=0)).astype(np.float32)
    if debug is not None:
        return out, np.concatenate(dbgs, axis=0)
    return out


def kernel(**inputs):
    inputs = {k: np.asarray(v) for k, v in inputs.items()}
    return run(inputs, 4, 8)
```

```python
import contextlib
import numpy as np
import concourse.bass as bass
import concourse.mybir as mybir
from concourse.bass_utils import run_bass_kernel_spmd

F32 = mybir.dt.float32
BF16 = mybir.dt.bfloat16
AF = mybir.ActivationFunctionType
ALU = mybir.AluOpType
AX = mybir.AxisListType

PE, ACT, DVE, POOL, SP = "tensor", "scalar", "vector", "gpsimd", "sync"
ENGS = [PE, ACT, DVE, POOL, SP]
SEM_ROLL = 30000
NDMASEM = 6
MIN_GAP = 3

T = 2048
D = 1024
NCH = 8
DFF = 2816
NF = 22
EPS = 1e-6


class Buf:
    __slots__ = ("name", "last_w", "readers")

    def __init__(self, name=""):
        self.name = name
        self.last_w = None
        self.readers = []


class Prog:
    def __init__(self, nc):
        self.nc = nc
        self.ops = {e: [] for e in ENGS}
        self.nsem = 0
        self.cur = {}
        self.cnt = {}
        for e in ENGS:
            self.cur[e] = self._newsem()
            self.cnt[e] = 0
        self.dsem = {e: [self._newsem() for _ in range(NDMASEM)] for e in (SP, ACT, POOL)}
        self.dcnt = {e: 0 for e in (SP, ACT, POOL)}
        self.waited = {e: {} for e in ENGS}
        self.nissued = {e: 0 for e in ENGS}
        self.final_dma = []

    def _newsem(self):
        k = self.nsem
        self.nsem += 1
        return k

    def _need(self, eng, dep, waits):
        if dep is None:
            return
        k, v, src = dep[0], dep[1], dep[2]
        if src == eng and eng != PE and len(dep) > 3 and self.nissued[eng] - dep[3] >= MIN_GAP:
            return
        if self.waited[eng].get(k, 0) >= v:
            return
        if waits.get(k, 0) < v:
            waits[k] = v

    def op(self, eng, fn, reads=(), writes=(), inc=True):
        waits = {}
        for b in reads:
            self._need(eng, b.last_w, waits)
        for b in writes:
            if b.last_w is not None and b.last_w[2] != eng:
                self._need(eng, b.last_w, waits)
            for r in b.readers:
                if r[2] != eng:
                    self._need(eng, r, waits)
        for k, v in waits.items():
            self.waited[eng][k] = v
        if self.cnt[eng] >= SEM_ROLL:
            self.cur[eng] = self._newsem()
            self.cnt[eng] = 0
        self.nissued[eng] += 1
        if inc:
            self.cnt[eng] += 1
            tok = (self.cur[eng], self.cnt[eng], eng, self.nissued[eng])
            incspec = (self.cur[eng], 1, self.cnt[eng])
        else:
            tok = (self.cur[eng], self.cnt[eng] + 1, eng, self.nissued[eng])
            incspec = None
        self.ops[eng].append((fn, list(waits.items()), incspec))
        for b in reads:
            b.readers.append(tok)
        for b in writes:
            b.last_w = tok
            b.readers = []
        return tok

    def dma(self, q, out, in_, reads=(), writes=(), final=False):
        waits = {}
        for b in reads:
            self._need(q, b.last_w, waits)
        for b in writes:
            self._need(q, b.last_w, waits)
            for r in b.readers:
                self._need(q, r, waits)
        i = self.dcnt[q]
        self.dcnt[q] += 1
        sem = self.dsem[q][i % NDMASEM]
        rnd = i // NDMASEM
        if rnd > 0:
            self._need(q, (sem, 16 * rnd, "dma"), waits)
        for k, v in waits.items():
            self.waited[q][k] = v
        tok = (sem, 16 * (rnd + 1), "dma")

        def fn(e, out=out, in_=in_):
            return e.dma_start(out=out, in_=in_)
        self.ops[q].append((fn, list(waits.items()), (sem, 16)))
        for b in reads:
            b.readers.append(tok)
        for b in writes:
            b.last_w = tok
            b.readers = []
        if final:
            self.final_dma.append((q, tok))
        return tok

    def barrier(self):
        targets = {}
        for e in ENGS:
            if self.cnt[e] > 0:
                targets[self.cur[e]] = (self.cnt[e], e)
        for q in self.dsem:
            for i in range(self.dcnt[q]):
                targets[self.dsem[q][i % NDMASEM]] = (16 * (i // NDMASEM + 1), "dma")
        for e in ENGS:
            waits = {}
            for k, (v, src) in targets.items():
                if self.waited[e].get(k, 0) < v:
                    waits[k] = v
                    self.waited[e][k] = v
            if waits:
                self.ops[e].append((None, list(waits.items()), None))

    def emit(self):
        nc = self.nc
        for q, tok in self.final_dma:
            self.ops[q].append((None, [(tok[0], tok[1])], None))
        dma_sems = set()
        for q in self.dsem:
            dma_sems.update(self.dsem[q])
        needed = {}
        for e in ENGS:
            for fn, waits, incspec in self.ops[e]:
                for k, v in waits:
                    if k not in dma_sems:
                        needed.setdefault(k, set()).add(v)
        remap = {}
        newcnt = {}
        for e in ENGS:
            for fn, waits, incspec in self.ops[e]:
                if incspec is not None and len(incspec) == 3:
                    k, _, v = incspec
                    if v in needed.get(k, ()):
                        newcnt[k] = newcnt.get(k, 0) + 1
                        remap[(k, v)] = newcnt[k]
        with contextlib.ExitStack() as st:
            sems = [st.enter_context(nc.semaphore("s%d" % i)) for i in range(self.nsem)]
            block = st.enter_context(nc.Block())

            def mk(engname):
                lst = self.ops[engname]

                def body(e):
                    for fn, waits, incspec in lst:
                        for k, v in waits:
                            if k in dma_sems:
                                e.wait_ge(sems[k], v)
                            else:
                                e.wait_ge(sems[k], remap[(k, v)])
                        if fn is None:
                            continue
                        ins = fn(e)
                        if incspec is not None:
                            if len(incspec) == 3:
                                if (incspec[0], incspec[2]) in remap:
                                    ins.then_inc(sems[incspec[0]], 1)
                            else:
                                ins.then_inc(sems[incspec[0]], incspec[1])
                return body
            for engname in ENGS:
                if self.ops[engname]:
                    getattr(block, engname)(mk(engname))
        return nc


class KB:
    def __init__(self, nc, st):
        self.nc = nc
        self.st = st
        self.P = Prog(nc)
        self.psb = []
        for i in range(8):
            t = st.enter_context(nc.psum_tensor("psb%d" % i, [128, 512], F32))
            self.psb.append((t, Buf("ps%d" % i)))

    def sb(self, name, shape, dt=F32):
        return self.st.enter_context(self.nc.sbuf_tensor(name, list(shape), dt))

    def mm(self, out, lhsT, rhs, start, stop, reads, writes, inc=True):
        self.P.op(PE, lambda e: e.matmul(out, lhsT=lhsT, rhs=rhs, start=start, stop=stop), reads, writes, inc)

    def tr(self, out, in_, ident, reads, writes, inc=True):
        self.P.op(PE, lambda e: e.transpose(out, in_, ident), reads, writes, inc)

    def act(self, out, in_, func, reads, writes, bias=None, scale=None, eng=ACT):
        kw = {}
        if bias is not None:
            kw["bias"] = bias
        if scale is not None:
            kw["scale"] = scale
        self.P.op(ACT, lambda e: e.activation(out=out, in_=in_, func=func, **kw), reads, writes)

    def tt(self, out, in0, in1, op, reads, writes, eng=DVE):
        self.P.op(eng, lambda e: e.tensor_tensor(out=out, in0=in0, in1=in1, op=op), reads, writes)

    def ts(self, out, in0, s1, s2, op0, op1, reads, writes, eng=DVE):
        if op1 is None:
            self.P.op(eng, lambda e: e.tensor_scalar(out=out, in0=in0, scalar1=s1, scalar2=None, op0=op0), reads, writes)
        else:
            self.P.op(eng, lambda e: e.tensor_scalar(out=out, in0=in0, scalar1=s1, scalar2=s2, op0=op0, op1=op1), reads, writes)

    def stt(self, out, in0, scalar, in1, op0, op1, reads, writes, eng=DVE):
        self.P.op(eng, lambda e: e.scalar_tensor_tensor(out=out, in0=in0, scalar=scalar, in1=in1, op0=op0, op1=op1), reads, writes)

    def cp(self, out, in_, reads, writes, eng=DVE):
        if eng == ACT:
            self.P.op(ACT, lambda e: e.copy(out=out, in_=in_), reads, writes)
        else:
            self.P.op(eng, lambda e: e.tensor_copy(out=out, in_=in_), reads, writes)

    def memset(self, ap, val, writes, eng=DVE):
        self.P.op(eng, lambda e: e.memset(ap, val), (), writes)


NOC = 17
AW = 32000


ALL_PARTS = ("conv", "s5prep", "ffn", "inproj", "attn", "ssm", "rwkv", "outproj")


def build(nseq, stop_after=99, n_layers=2, debug=None, parts=ALL_PARTS):
    nc = bass.Bass("TRN2", target_bir_lowering=False)
    xT = nc.dram_tensor("xT", [nseq, NCH, 128, T], F32, kind="ExternalInput").ap()
    outT = nc.dram_tensor("outT", [nseq, NCH, 128, T], F32, kind="ExternalOutput").ap()
    wgu = nc.dram_tensor("wgu", [4, 2, DFF, D], F32, kind="ExternalInput").ap()
    wdn = nc.dram_tensor("wdn", [4, D, DFF], F32, kind="ExternalInput").ap()
    win = nc.dram_tensor("win", [2, NOC * 128, D], F32, kind="ExternalInput").ap()
    wout = nc.dram_tensor("wout", [2, D, D], F32, kind="ExternalInput").ap()
    gains = nc.dram_tensor("gains", [128, 12, NCH], F32, kind="ExternalInput").ap()
    ident_d = nc.dram_tensor("ident", [128, 128], F32, kind="ExternalInput").ap()
    biasT_d = nc.dram_tensor("biasT", [128, 2, 6, 128], F32, kind="ExternalInput").ap()
    maskT_d = nc.dram_tensor("maskT", [128, 2, 128], F32, kind="ExternalInput").ap()
    sinks_d = nc.dram_tensor("sinks", [128, 2, 6], F32, kind="ExternalInput").ap()
    wgu_b = nc.dram_tensor("wgu_b", [4, 2, DFF, D], BF16, kind="Internal").ap()
    wdn_b = nc.dram_tensor("wdn_b", [4, D, DFF], BF16, kind="Internal").ap()
    win_b = nc.dram_tensor("win_b", [2, NOC * 128, D], BF16, kind="Internal").ap()
    wout_b = nc.dram_tensor("wout_b", [2, D, D], BF16, kind="Internal").ap()
    pT = nc.dram_tensor("pT", [NOC, 128, T], F32, kind="Internal").ap()
    rwp_d = nc.dram_tensor("rwp", [128, 2, 31], F32, kind="ExternalInput").ap()
    lw_d = nc.dram_tensor("lw", [128, 2, 384], F32, kind="ExternalInput").ap()
    mU_d = nc.dram_tensor("mU", [128, 2, 128], F32, kind="ExternalInput").ap()
    mL_d = nc.dram_tensor("mL", [128, 128], F32, kind="ExternalInput").ap()
    BD_d = nc.dram_tensor("BD", [128, 128], F32, kind="ExternalInput").ap()
    rowm_d = nc.dram_tensor("rowm", [128, 2], F32, kind="ExternalInput").ap()
    s5p_d = nc.dram_tensor("s5p", [128, 2, 3, 8], F32, kind="ExternalInput").ap()
    bb_d = nc.dram_tensor("bb", [128, 2, 2, 8, 128], F32, kind="ExternalInput").ap()
    cc_d = nc.dram_tensor("cc", [128, 2, 2, 8, 128], F32, kind="ExternalInput").ap()
    s5v_d = nc.dram_tensor("s5v", [128, 2, 2, 2], F32, kind="ExternalInput").ap()
    gluw_d = nc.dram_tensor("gluw", [128, 2, 2, 2, 128], F32, kind="ExternalInput").ap()
    tabs = nc.dram_tensor("tabs", [2, 8, 128, 4, 512], F32, kind="Internal").ap()
    dbg = None
    if debug is not None:
        dbg = nc.dram_tensor("dbg", [nseq, NCH, 128, T], F32, kind="ExternalOutput").ap()

    with contextlib.ExitStack() as st:
        kb = KB(nc, st)
        P = kb.P
        h = kb.sb("h", [128, NCH, T], F32)
        hB = [Buf("h%d" % i) for i in range(4)]
        ones_b = kb.sb("ones_b", [128, 128], BF16)
        onesB = Buf("ones")
        kb.memset(ones_b[:], 1.0, [onesB])
        ident = kb.sb("identsb", [128, 128], F32)
        identB = Buf("ident")
        P.dma(SP, ident[:], ident_d[:, :], writes=[identB])
        g_sb = kb.sb("g_sb", [128, 12, NCH], F32)
        gB = Buf("g")
        P.dma(SP, g_sb[:], gains[:, :, :], writes=[gB])
        g32 = kb.sb("g32", [128, 12, NCH], F32)
        g32B = Buf("g32")
        for idx in range(12):
            k6 = idx % 6
            sc = 16.0 if k6 in (1, 5) else 32.0
            kb.ts(g32[:, idx, :], g_sb[:, idx, :], sc, None, ALU.mult, None, [gB], [g32B])
        epsb = kb.sb("epsb", [128, 1], F32)
        epsB = Buf("epsb")
        kb.memset(epsb[:], float(D * EPS), [epsB])
        biasT = kb.sb("biasT_sb", [128, 2, 6, 128], F32)
        biasB = Buf("biasT")
        maskT = kb.sb("maskT_sb", [128, 2, 128], F32)
        maskB = Buf("maskT")
        P.dma(SP, biasT[:], biasT_d[:, :, :, :], writes=[biasB])
        P.dma(SP, maskT[:], maskT_d[:, :, :], writes=[maskB])
        for hh in range(6):
            kb.tt(biasT[:, :, hh, :], biasT[:, :, hh, :], maskT[:], ALU.add, [biasB, maskB], [biasB])
        esink = kb.sb("esink", [128, 2, 6], F32)
        esinkB = Buf("esink")
        P.dma(SP, esink[:], sinks_d[:, :, :], writes=[esinkB])
        kb.act(esink[:], esink[:], AF.Exp, [esinkB], [esinkB])

        rwp = kb.sb("rwp_sb", [128, 2, 31], F32); rwpB = Buf("rwp")
        P.dma(SP, rwp[:], rwp_d[:, :, :], writes=[rwpB])
        lw = kb.sb("lw_sb", [128, 2, 384], F32); lwB = Buf("lw")
        P.dma(SP, lw[:], lw_d[:, :, :], writes=[lwB])
        mU = kb.sb("mU_sb", [128, 2, 128], F32); mUB = Buf("mU")
        P.dma(SP, mU[:], mU_d[:, :, :], writes=[mUB])
        mL = kb.sb("mL_sb", [128, 128], F32); mLB = Buf("mL")
        P.dma(SP, mL[:], mL_d[:, :], writes=[mLB])
        BD = kb.sb("BD_sb", [128, 128], F32); BDB = Buf("BD")
        P.dma(SP, BD[:], BD_d[:, :], writes=[BDB])
        rowm = kb.sb("rowm_sb", [128, 2], F32); rowmB = Buf("rowm")
        P.dma(SP, rowm[:], rowm_d[:, :], writes=[rowmB])
        rwd = kb.sb("rwd", [128, 2, 16], F32); rwdB = Buf("rwd")
        kb.ts(rwd[:, :, 0:10], rwp[:, :, 0:10], -1.0, 1.0, ALU.mult, ALU.add, [rwpB], [rwdB])
        kb.ts(rwd[:, :, 10:13], rwp[:, :, 10:13], -1.0, None, ALU.mult, None, [rwpB], [rwdB])
        kb.ts(rwd[:, :, 13:16], rwp[:, :, 19:22], -1.0, 1.0, ALU.mult, ALU.add, [rwpB], [rwdB])
        cst = kb.sb("cst", [128, 4], F32); cstB = Buf("cst")
        kb.memset(cst[:, 0:1], 1.0, [cstB])
        kb.memset(cst[:, 1:2], -0.5, [cstB])
        kb.memset(cst[:, 2:3], 1e-24, [cstB])
        kb.memset(cst[:, 3:4], 64e-5, [cstB])
        ones128 = kb.sb("ones128", [128, 128], F32); ones128B = Buf("ones128")
        kb.memset(ones128[:], 1.0, [ones128B])
        s5p = kb.sb("s5p_sb", [128, 2, 3, 8], F32); s5pB = Buf("s5p")
        P.dma(SP, s5p[:], s5p_d[:, :, :, :], writes=[s5pB])
        s5v = kb.sb("s5v_sb", [128, 2, 2, 2], F32); s5vB = Buf("s5v")
        P.dma(SP, s5v[:], s5v_d[:, :, :, :], writes=[s5vB])
        gluw = kb.sb("gluw_sb", [128, 2, 2, 2, 128], F32); gluwB = Buf("gluw")
        P.dma(SP, gluw[:], gluw_d[:, :, :, :, :], writes=[gluwB])
        rho = kb.sb("rho", [128, 2, 8], F32); rhoB = Buf("rho")
        hpi = kb.sb("hpi", [128, 1], F32); hpiB = Buf("hpi")
        kb.memset(hpi[:], float(np.pi / 2), [hpiB])

        wB = Buf("wconv")
        for fi in range(2 * n_layers if "conv" in parts else 0):
            for m in range(2):
                for q4 in range(4):
                    r0 = q4 * (DFF // 4)
                    P.dma(POOL, wgu_b[fi, m, r0:r0 + DFF // 4, :], wgu[fi, m, r0:r0 + DFF // 4, :], writes=[wB])
            for q4 in range(4):
                r0 = q4 * (D // 4)
                P.dma(POOL, wdn_b[fi, r0:r0 + D // 4, :], wdn[fi, r0:r0 + D // 4, :], writes=[wB])
        for l in range(n_layers if "conv" in parts else 0):
            for q4 in range(NOC):
                P.dma(POOL, win_b[l, q4 * 128:(q4 + 1) * 128, :], win[l, q4 * 128:(q4 + 1) * 128, :], writes=[wB])
            for q4 in range(4):
                P.dma(POOL, wout_b[l, q4 * 256:(q4 + 1) * 256, :], wout[l, q4 * 256:(q4 + 1) * 256, :], writes=[wB])
        P.barrier()
        convB = Buf("conv")

        arena = kb.sb("arena", [128, AW], F32)

        class Carver:
            def __init__(self):
                self.off = 0

            def f32(self, shape):
                n = int(np.prod(shape))
                ap = arena[:, self.off:self.off + n]
                self.off += n
                assert self.off <= AW, self.off
                return self._shape(ap, shape)

            def bf16(self, shape):
                n = int(np.prod(shape))
                nw = (n + 1) // 2
                ap = arena[:, self.off:self.off + nw].bitcast(BF16)
                self.off += nw
                assert self.off <= AW, self.off
                if 2 * nw != n:
                    ap = ap[:, 0:n]
                return self._shape(ap, shape)

            @staticmethod
            def _shape(ap, shape):
                if len(shape) == 1:
                    return ap
                if len(shape) == 2:
                    return ap.rearrange("p (a b) -> p a b", b=shape[1])
                if len(shape) == 3:
                    return ap.rearrange("p (a b c) -> p a b c", b=shape[1], c=shape[2])
                raise ValueError(shape)

        def s5_prep(l):
            cp_ = Carver()
            sm = [cp_.f32([8]) for i in range(16)]
            smB = Buf("s5small")
            Ec = cp_.f32([8, 512]); Es = cp_.f32([8, 512]); EB = Buf("E")
            Tc = cp_.f32([8, 512]); Tn = cp_.f32([8, 512]); TB = Buf("T")
            t1 = cp_.f32([8, 512]); t2 = cp_.f32([8, 512]); tB = Buf("t12")
            lr, dt, th, c_, s_, c2, s2, u1, u2, nr, fr, fi, den, li = sm[:14]
            R, W = [s5pB, smB], [smB]
            kb.ts(lr, s5p[:, l, 0, :], -1e-4, None, ALU.min, None, R, W)
            kb.cp(li, s5p[:, l, 1, :], R, W)
            kb.act(dt, s5p[:, l, 2, :], AF.Exp, R, W)
            kb.tt(u1, lr, dt, ALU.mult, R, W)
            kb.act(rho[:, l, :], u1, AF.Exp, R, [rhoB])
            kb.tt(th, li, dt, ALU.mult, R, W)
            kb.act(s_, th, AF.Sin, R, W, scale=1.0 / 64)
            kb.act(c_, th, AF.Sin, R + [hpiB], W, scale=1.0 / 64, bias=hpi[:, 0:1])
            for it in range(6):
                kb.tt(u1, c_, c_, ALU.mult, R, W)
                kb.tt(u2, s_, s_, ALU.mult, R, W)
                kb.tt(s2, c_, s_, ALU.mult, R, W)
                kb.tt(c2, u1, u2, ALU.subtract, R, W)
                kb.ts(s_, s2, 2.0, None, ALU.mult, None, R, W)
                kb.cp(c_, c2, R, W)
            kb.tt(u1, rho[:, l, :], c_, ALU.mult, R + [rhoB], W)
            kb.tt(u2, rho[:, l, :], s_, ALU.mult, R + [rhoB], W)
            kb.ts(nr, u1, -1.0, None, ALU.add, None, R, W)
            kb.tt(den, lr, lr, ALU.mult, R, W)
            kb.tt(c2, li, li, ALU.mult, R, W)
            kb.tt(den, den, c2, ALU.add, R, W)
            kb.P.op(DVE, lambda e: e.reciprocal(out=den, in_=den), R, W)
            kb.tt(c2, nr, lr, ALU.mult, R, W)
            kb.tt(s2, u2, li, ALU.mult, R, W)
            kb.tt(fr, c2, s2, ALU.add, R, W)
            kb.tt(fr, fr, den, ALU.mult, R, W)
            kb.tt(c2, u2, lr, ALU.mult, R, W)
            kb.tt(s2, nr, li, ALU.mult, R, W)
            kb.tt(fi, c2, s2, ALU.subtract, R, W)
            kb.tt(fi, fi, den, ALU.mult, R, W)
            RE = [smB, EB, tB]
            kb.cp(Ec[:, :, 0:1], c_.unsqueeze(2), RE, [EB])
            kb.cp(Es[:, :, 0:1], s_.unsqueeze(2), RE, [EB])
            n = 1
            while n < 512:
                bc = Ec[:, :, n - 1:n].to_broadcast([128, 8, n])
                bs = Es[:, :, n - 1:n].to_broadcast([128, 8, n])
                kb.tt(t1[:, :, 0:n], Ec[:, :, 0:n], bc, ALU.mult, RE, [tB])
                kb.tt(t2[:, :, 0:n], Es[:, :, 0:n], bs, ALU.mult, RE, [tB])
                kb.tt(Ec[:, :, n:2 * n], t1[:, :, 0:n], t2[:, :, 0:n], ALU.subtract, RE, [EB])
                kb.tt(t1[:, :, 0:n], Ec[:, :, 0:n], bs, ALU.mult, RE, [tB])
                kb.tt(t2[:, :, 0:n], Es[:, :, 0:n], bc, ALU.mult, RE, [tB])
                kb.tt(Es[:, :, n:2 * n], t1[:, :, 0:n], t2[:, :, 0:n], ALU.add, RE, [EB])
                n *= 2
            frb = fr.unsqueeze(2).to_broadcast([128, 8, 512])
            fib = fi.unsqueeze(2).to_broadcast([128, 8, 512])
            kb.tt(t1, Ec, frb, ALU.mult, RE, [tB])
            kb.tt(t2, Es, fib, ALU.mult, RE, [tB])
            kb.tt(Tc, t1, t2, ALU.add, RE, [TB])
            kb.tt(t1, Ec, fib, ALU.mult, RE, [tB])
            kb.tt(t2, Es, frb, ALU.mult, RE, [tB])
            kb.tt(Tn, t1, t2, ALU.subtract, RE, [TB])
            for j in range(8):
                for a, src in enumerate((Ec, Es, Tc, Tn)):
                    P.dma(SP, tabs[l, j, :, a, :], src[:, j, :], reads=[EB, TB])
            P.barrier()

        for l in range(n_layers if "s5prep" in parts else 0):
            s5_prep(l)

        cv = Carver()
        sq = cv.bf16([NCH, 512]); sqB = Buf("sq")
        rstd = cv.f32([512]); rstdB = Buf("rstd")
        rstd0 = cv.f32([512]); rstd0B = Buf("rstd0")
        xn = cv.bf16([NCH, 512]); xnB = Buf("xn")
        xnb_ = cv.bf16([NCH, 512]); xnbB = Buf("xnb")
        xn2 = [xn, xnb_]; xn2B = [xnB, xnbB]
        sqp = cv.bf16([NCH, 512]); sqpB = Buf("sqp")
        rstdp = cv.f32([512]); rstdpB = Buf("rstdp")
        rstd0p = cv.f32([512]); rstd0pB = Buf("rstd0p")
        actb = cv.bf16([NF, 512]); actB = [Buf("act%d" % f) for f in range(NF)]
        fout = cv.f32([NCH, 512]); foutB = Buf("fout")
        sg = [cv.f32([512]) for i in range(2)]; sgB = [Buf("sg%d" % i) for i in range(2)]
        tmp = [cv.f32([512]) for i in range(2)]; tmpB = [Buf("tmp%d" % i) for i in range(2)]
        NWS = 3
        wgus = [cv.bf16([2, D]) for i in range(NWS)]; wgusB = [Buf("wgus%d" % i) for i in range(NWS)]
        wds = [cv.bf16([DFF]) for i in range(2)]; wdsB = [Buf("wds%d" % i) for i in range(2)]
        cnt = {"w": 0, "d": 0, "g": 0, "o": 0, "s": 0, "t": 0, "e": 0}

        def rms_rstd(src_fn, srcB, rstd_, rstdB_, rstd0_, rstd0B_, nchunks=NCH):
            ps, psB = kb.psb[0]
            for c in range(nchunks):
                kb.mm(ps[:], ones_b[:], src_fn(c), c == 0, c == nchunks - 1, [onesB, srcB], [psB], inc=(c == nchunks - 1))
            kb.act(rstd0_, ps[:], AF.Sqrt, [psB, epsB], [rstd0B_], bias=epsb[:, 0:1])
            kb.P.op(DVE, lambda e: e.reciprocal(out=rstd_, in_=rstd0_), [rstd0B_], [rstdB_])

        def ffn(fi, gpre, gpost):
            def prenorm(tt):
                t0 = tt * 512
                hb = hB[tt]
                xb, xbB = xn2[tt % 2], xn2B[tt % 2]
                kb.act(sqp, h[:, :, t0:t0 + 512], AF.Square, [hb], [sqpB])
                rms_rstd(lambda c: sqp[:, c, :], sqpB, rstdp, rstdpB, rstd0p, rstd0pB)
                for c in range(NCH):
                    kb.stt(xb[:, c, :], h[:, c, t0:t0 + 512], g32[:, gpre, c:c + 1], rstdp, ALU.mult, ALU.mult,
                           [hb, g32B, rstdpB], [xbB])

            def gateup(tt):
                xb, xbB = xn2[tt % 2], xn2B[tt % 2]
                for f in range(NF):
                    ws = cnt["w"] % NWS
                    cnt["w"] += 1
                    P.dma(SP, wgus[ws], wgu_b[fi, :, f * 128:(f + 1) * 128, :].rearrange("m p k -> p m k"),
                          reads=[convB], writes=[wgusB[ws]])
                    pg, pgB = kb.psb[1 + cnt["g"] % 2]
                    pu, puB = kb.psb[3 + cnt["g"] % 2]
                    cnt["g"] += 1
                    for c in range(NCH):
                        kb.mm(pg[:], wgus[ws][:, 0, c * 128:(c + 1) * 128], xb[:, c, :], c == 0, c == NCH - 1,
                              [wgusB[ws], xbB], [pgB], inc=(c == NCH - 1))
                    for c in range(NCH):
                        kb.mm(pu[:], wgus[ws][:, 1, c * 128:(c + 1) * 128], xb[:, c, :], c == 0, c == NCH - 1,
                              [wgusB[ws], xbB], [puB], inc=(c == NCH - 1))
                    si = cnt["s"] % 2
                    cnt["s"] += 1
                    kb.act(sg[si], pg[:], AF.Silu, [pgB], [sgB[si]])
                    kb.tt(actb[:, f, :], sg[si], pu[:], ALU.mult, [sgB[si], puB], [actB[f]])

            def down_post(tt):
                t0 = tt * 512
                hb = hB[tt]
                for dc in range(NCH):
                    di = cnt["d"] % 2
                    cnt["d"] += 1
                    P.dma(SP, wds[di], wdn_b[fi, dc * 128:(dc + 1) * 128, :], reads=[convB], writes=[wdsB[di]])
                    po, poB = kb.psb[5 + cnt["o"] % 2]
                    cnt["o"] += 1
                    for f in range(NF):
                        kb.mm(po[:], wds[di][:, f * 128:(f + 1) * 128], actb[:, f, :], f == 0, f == NF - 1,
                              [wdsB[di], actB[f]], [poB], inc=(f == NF - 1))
                    kb.cp(fout[:, dc, :], po[:], [poB], [foutB], eng=ACT)
                    kb.act(sq[:, dc, :], po[:], AF.Square, [poB], [sqB])
                rms_rstd(lambda c: sq[:, c, :], sqB, rstd, rstdB, rstd0, rstd0B)
                for c in range(NCH):
                    ti = cnt["t"] % 2
                    cnt["t"] += 1
                    kb.stt(tmp[ti], fout[:, c, :], g32[:, gpost, c:c + 1], rstd, ALU.mult, ALU.mult,
                           [foutB, g32B, rstdB], [tmpB[ti]])
                    kb.tt(h[:, c, t0:t0 + 512], h[:, c, t0:t0 + 512], tmp[ti], ALU.add, [hb, tmpB[ti]], [hb], eng=POOL)

            prenorm(0)
            for tt in range(4):
                gateup(tt)
                if tt < 3:
                    prenorm(tt + 1)
                down_post(tt)

        cm = Carver()
        ycat = cm.bf16([NCH, T]); ycatB = [Buf("ycat%d" % i) for i in range(NCH)]
        MB1 = cm.off
        xnT = cm.bf16([NCH, T]); xnTB = [Buf("xnT%d" % i) for i in range(4)]
        MB0 = cm.off

        pTB = [Buf("pT%d" % i) for i in range(NOC)]

        def mixer_inproj(l):
            cm.off = MB0
            msq = cm.bf16([NCH, 512]); msqB = Buf("msq")
            mr = cm.f32([512]); mrB = Buf("mr")
            mr0 = cm.f32([512]); mr0B = Buf("mr0")
            wsl = [cm.bf16([D]) for i in range(3)]; wslB = [Buf("wsl%d" % i) for i in range(3)]
            stg = [cm.f32([512]) for i in range(3)]; stgB = [Buf("stg%d" % i) for i in range(3)]
            gpre = 6 * l + 2
            for tt in range(4):
                t0 = tt * 512
                kb.act(msq, h[:, :, t0:t0 + 512], AF.Square, [hB[tt]], [msqB])
                rms_rstd(lambda c: msq[:, c, :], msqB, mr, mrB, mr0, mr0B)
                for c in range(NCH):
                    kb.stt(xnT[:, c, t0:t0 + 512], h[:, c, t0:t0 + 512], g32[:, gpre, c:c + 1], mr, ALU.mult, ALU.mult,
                           [hB[tt], g32B, mrB], [xnTB[tt]])
            k = 0
            for oc in range(NOC):
                if oc == 4:
                    continue
                wi = oc % 3
                P.dma(SP, wsl[wi], win_b[l, oc * 128:(oc + 1) * 128, :], reads=[convB], writes=[wslB[wi]])
                for tt in range(4):
                    t0 = tt * 512
                    ps, psB = kb.psb[1 + k % 4]
                    si = k % 3
                    k += 1
                    for c in range(NCH):
                        kb.mm(ps[:], wsl[wi][:, c * 128:(c + 1) * 128], xnT[:, c, t0:t0 + 512], c == 0, c == NCH - 1,
                              [wslB[wi], xnTB[tt]], [psB], inc=(c == NCH - 1))
                    kb.cp(stg[si], ps[:], [psB], [stgB[si]], eng=(ACT if k % 2 else DVE))
                    P.dma(SP, pT[oc, :, t0:t0 + 512], stg[si], reads=[stgB[si]], writes=[pTB[oc]])

        def mixer_attn(l):
            cm.off = MB0
            qT = cm.bf16([3, T]); qTBs = [Buf("qT%d" % i) for i in range(3)]
            kT = cm.bf16([T]); kTB = Buf("kT")
            wv = cm.bf16([D]); wvB = Buf("wv")
            vaug = cm.bf16([16, 2 * 66]); vaugB = Buf("vaug")
            sT = [cm.f32([384]) for i in range(4)]; sTB = [Buf("sT%d" % i) for i in range(4)]
            pE = [cm.bf16([3, 128]) for i in range(4)]; pEB = [Buf("pE%d" % i) for i in range(4)]
            den = cm.f32([6]); denB = Buf("den")
            yat = cm.f32([6, 64]); yatB = Buf("yat")
            for g in range(3):
                P.dma(POOL, qT[:, g, :], pT[g], reads=[pTB[g]], writes=[qTBs[g]])
            P.dma(POOL, kT, pT[3], reads=[pTB[3]], writes=[kTB])
            P.dma(SP, wv, win_b[l, 4 * 128:5 * 128, :], reads=[convB], writes=[wvB])
            kb.memset(vaug, 1.0, [vaugB])
            for n in range(16):
                ps, psB = kb.psb[1 + n % 2]
                for c in range(NCH):
                    kb.mm(ps[:, 0:128], xnT[:, c, n * 128:(n + 1) * 128], wv[:, c * 128:(c + 1) * 128], c == 0, c == NCH - 1,
                          [xnTB[n // 4], wvB], [psB], inc=(c == NCH - 1))
                vdst = vaug[:, n, :].rearrange("p (k d) -> p k d", d=66)[:, :, 0:64]
                kb.cp(vdst, ps[:, 0:128].rearrange("p (k d) -> p k d", d=64), [psB], [vaugB], eng=ACT)
            for n in range(16):
                kbs = [0, 1] if n > 0 else [1]
                for kvh in range(2):
                    r0 = kvh * 64
                    for kbi in kbs:
                        idx = kvh * 2 + kbi
                        ps, psB = kb.psb[1 + idx]
                        kblk = n - 1 + kbi
                        kb.mm(ps[:, 0:384], kT[r0:r0 + 64, kblk * 128:(kblk + 1) * 128],
                              qT[r0:r0 + 64, :, n * 128:(n + 1) * 128], True, True, [kTB] + qTBs, [psB])
                        kb.stt(sT[idx], ps[:, 0:384], 0.125,
                               biasT[:, kbi, kvh * 3:(kvh + 1) * 3, :].rearrange("p a b -> p (a b)"),
                               ALU.mult, ALU.add, [psB, biasB], [sTB[idx]])
                        kb.act(pE[idx], sT[idx].rearrange("p (a b) -> p a b", b=128), AF.Exp, [sTB[idx]], [pEB[idx]])
                po, poB = kb.psb[5 + n % 2]
                pov = po[:, 0:6 * 66].rearrange("p (a b) -> p a b", b=66)
                for kvh in range(2):
                    for g in range(3):
                        hh = kvh * 3 + g
                        for j, kbi in enumerate(kbs):
                            idx = kvh * 2 + kbi
                            kblk = n - 1 + kbi
                            kb.mm(pov[:, hh, 0:65], pE[idx][:, g, :], vaug[:, kblk, kvh * 66:kvh * 66 + 65],
                                  j == 0, j == len(kbs) - 1, [pEB[idx], vaugB], [poB],
                                  inc=(hh == 5 and j == len(kbs) - 1))
                kb.tt(den, pov[:, :, 64], esink[:, l, :], ALU.add, [poB, esinkB], [denB])
                kb.P.op(DVE, lambda e: e.reciprocal(out=den, in_=den), [denB], [denB])
                kb.tt(yat, pov[:, :, 0:64], den.unsqueeze(2).to_broadcast([128, 6, 64]), ALU.mult, [poB, denB], [yatB])
                pt, ptB = kb.psb[7]
                for j in range(3):
                    kb.tr(pt[:, j * 128:(j + 1) * 128], yat[:, 2 * j:2 * j + 2, :].rearrange("p a b -> p (a b)"), ident[:],
                          [yatB, identB], [ptB], inc=(j == 2))
                kb.cp(ycat[:, 0:3, n * 128:(n + 1) * 128], pt[:, 0:384].rearrange("p (a b) -> p a b", b=128),
                      [ptB], [ycatB[0], ycatB[1], ycatB[2]], eng=ACT)

        def mixer_rwkv(l):
            import os
            RWCUT = float(os.environ.get("RWCUT", "9"))
            cm.off = MB1
            ST = [[cm.f32([64]) for i in range(2)] for jj in range(3)]
            STB = [[Buf("ST%d%d" % (jj, i)) for i in range(2)] for jj in range(3)]
            scur = [0, 0, 0]
            o_pcpp = cm.off
            pcpp = cm.f32([22, 256]); pcB = Buf("pc"); ppB = Buf("pp")
            pc = pcpp[:, 0:10, :]; pp = pcpp[:, 11:21, :]
            lgw = pcpp[:, 0:3, :]; asig = pcpp[:, 3:6, :]; g_ = pcpp[:, 6:9, :]
            kk = pcpp[:, 11:14, :]; kmod = pcpp[:, 14:17, :]; a_ = pcpp[:, 17:20, :]
            o_psh = cm.off
            psh = cm.f32([10, 256]); pshB = Buf("psh")
            e1 = cm.f32([3, 256]); e1B = Buf("e1")
            o_e2 = cm.off
            e2 = cm.f32([3, 256]); e2B = Buf("e2")
            e2b = cm.f32([3, 256]); e2bB = Buf("e2b")
            b_ = cm.f32([3, 256]); bonus = cm.f32([3, 256])
            o_lP = cm.off
            lP = cm.f32([3, 256]); eP = cm.f32([3, 256])
            AR = cm.f32([3, 2 * 2 * 128]); bT = cm.f32([3, 256]); kT_ = cm.f32([3, 256])
            bc = cm.f32([3, 256]); kc = cm.f32([3, 256])
            yfm = e1
            GB = Buf("rwkv_pre")
            o_core = cm.off
            cm.f32([4300])
            ext_lists = [
                [[o_core, 4300]],
                [[o_pcpp + 9 * 256, 13 * 256], [o_lP, 768], [o_psh + 9 * 256, 256]],
                [[o_pcpp, 1536], [o_psh, 1536], [o_e2, 2304]],
            ]

            def ext_alloc(exts, shape):
                n = int(np.prod(shape))
                for e_ in exts:
                    if e_[1] >= n:
                        ap = arena[:, e_[0]:e_[0] + n]
                        e_[0] += n
                        e_[1] -= n
                        return Carver._shape(ap, shape)
                raise AssertionError("extent alloc failed %s" % (shape,))
            INST = []
            o_free = cm.off
            free_ext = [[o_free, AW - o_free]]
            YtA = ext_alloc(free_ext, [2, 3 * 128]); YtAB = Buf("YtA")
            for k_ in range(3):
                ex = ext_lists[k_]
                d = {}
                for nm, shp in (("TM", [4, 128]), ("AbT", [2, 256]), ("AkT", [2, 256]), ("NN0", [4, 128]), ("NN1", [4, 128]),
                                ("Nm", [2, 128]), ("R0", [2, 128]), ("R1", [2, 128]), ("RhT", [2, 128]), ("Phi", [128]),
                                ("Ytok", [2, 64]), ("cen", [2, 64]), ("sqv", [2, 64]), ("Z", [64]), ("yn", [2, 64]),
                                ("st2", [2]), ("st3", [2])):
                    d[nm] = ext_alloc(ex, shp)
                    d[nm + "B"] = Buf(nm + str(k_))
                INST.append(d)
            for jj in range(3):
                kb.memset(ST[jj][0], 0.0, [STB[jj][0]])
            def bcp(col0, n):
                return rwp[:, l, col0:col0 + n].unsqueeze(2).to_broadcast([128, n, 256])
            def bcd(col0, n):
                return rwd[:, l, col0:col0 + n].unsqueeze(2).to_broadcast([128, n, 256])
            RP = [GB, rwpB, rwdB, cstB]
            for tile in range(8):
                t0 = tile * 256
                P.barrier()
                src = pT[5:15, :, t0:t0 + 256].rearrange("c p t -> p c t")
                P.dma(SP, pc, src, reads=[pTB[5]], writes=[pcB])
                if tile == 0:
                    kb.memset(pp[:, :, 0:1], 0.0, [ppB])
                    P.dma(SP, pp[:, :, 1:256], pT[5:15, :, 0:255].rearrange("c p t -> p c t"), reads=[pTB[5]], writes=[ppB])
                else:
                    P.dma(SP, pp, pT[5:15, :, t0 - 1:t0 + 255].rearrange("c p t -> p c t"), reads=[pTB[5]], writes=[ppB])
                P.barrier()
                kb.tt(pp, pp, bcp(0, 10), ALU.mult, RP, [GB])
                kb.tt(pc, pc, bcd(0, 10), ALU.mult, RP, [GB])
                kb.tt(psh, pc, pp, ALU.add, RP, [GB])
                rr = psh[:, 0:3, :]; kraw = psh[:, 3:6, :]; vv = psh[:, 6:9, :]
                tw = e2[:, 0, :]; sgg = e2[:, 1, :]
                kb.act(tw[0:32, :], psh[0:32, 9, :], AF.Tanh, RP, [GB])
                kb.act(sgg[64:128, :], psh[64:128, 9, :], AF.Sigmoid, RP, [GB])
                def lbank(base, jj):
                    t_, tB_ = kb.psb[base + (1 if jj == 2 else 0)]
                    c0 = 256 if jj == 1 else 0
                    return t_[:, c0:c0 + 256], tB_
                for jj in range(3):
                    cs_ = slice(jj * 128, (jj + 1) * 128)
                    pw, pwB = lbank(1, jj)
                    kb.mm(pw, lw[0:32, l, cs_], tw[0:32, :], True, True, RP + [lwB], [pwB])
                for jj in range(3):
                    cs_ = slice(jj * 128, (jj + 1) * 128)
                    pa, paB = lbank(3, jj)
                    kb.mm(pa, lw[32:64, l, cs_], psh[32:64, 9, :], True, True, RP + [lwB], [paB])
                for jj in range(3):
                    cs_ = slice(jj * 128, (jj + 1) * 128)
                    pg, pgB = lbank(5, jj)
                    kb.mm(pg, lw[64:128, l, cs_], sgg[64:128, :], True, True, RP + [lwB], [pgB])
                for jj in range(3):
                    pw, pwB = lbank(1, jj)
                    kb.act(e1[:, jj, :], pw, AF.Exp, [pwB] + RP, [GB], scale=-1.0, bias=rwd[:, l, 10 + jj:11 + jj])
                for jj in range(3):
                    pa, paB = lbank(3, jj)
                    kb.act(asig[:, jj, :], pa, AF.Sigmoid, [paB] + RP, [GB], bias=rwp[:, l, 13 + jj:14 + jj])
                for jj in range(3):
                    pg, pgB = lbank(5, jj)
                    kb.cp(g_[:, jj, :], pg, [pgB] + RP, [GB])
                kb.act(e1, e1, AF.Ln, RP, [GB], bias=cst[:, 0:1])
                kb.act(e1, e1, AF.Exp, RP, [GB], scale=-1.0, bias=cst[:, 1:2])
                kb.ts(lgw, e1, -1.0, None, ALU.mult, None, RP, [GB])
                kb.tt(kk, kraw, bcp(16, 3), ALU.mult, RP, [GB])
                kb.tt(e2, kk, kk, ALU.mult, RP, [GB])
                def nbank(jj):
                    t_, tB_ = kb.psb[7 if jj < 2 else 2]
                    c0 = 256 if jj >= 1 else 0
                    return t_[:, c0:c0 + 256], tB_
                for jj in range(3):
                    pn_, pnB_ = nbank(jj)
                    kb.mm(pn_, BD[:], e2[:, jj, :], True, True, RP + [BDB], [pnB_])
                for jj in range(3):
                    pn_, pnB_ = nbank(jj)
                    kb.act(e2b[:, jj, :], pn_, AF.Sqrt, [pnB_] + RP, [GB], bias=cst[:, 2:3])
                kb.P.op(DVE, lambda e: e.reciprocal(out=e2b, in_=e2b), RP, [GB])
                kb.tt(kk, kk, e2b, ALU.mult, RP, [GB])
                kb.tt(kmod, asig, bcp(19, 3), ALU.mult, RP, [GB])
                kb.tt(kmod, kmod, bcd(13, 3), ALU.add, RP, [GB])
                kb.tt(kmod, kmod, kraw, ALU.mult, RP, [GB])
                kb.ts(a_, kk, -1.0, None, ALU.mult, None, RP, [GB])
                kb.tt(b_, kk, asig, ALU.mult, RP, [GB])
                kb.tt(e2, rr, kmod, ALU.mult, RP, [GB])
                kb.tt(e2, e2, bcp(22, 3), ALU.mult, RP, [GB])
                def bbank(jj):
                    t_, tB_ = kb.psb[4 if jj < 2 else 6]
                    c0 = 256 if jj >= 1 else 0
                    return t_[:, c0:c0 + 256], tB_
                for jj in range(3):
                    pb_, pbB_ = bbank(jj)
                    kb.mm(pb_, BD[:], e2[:, jj, :], True, True, RP + [BDB], [pbB_])
                for jj in range(3):
                    pb_, pbB_ = bbank(jj)
                    kb.tt(bonus[:, jj, :], pb_, vv[:, jj, :], ALU.mult, [pbB_] + RP, [GB])
                for jj in range(3):
                    for c in range(2):
                        cs = slice(c * 128, (c + 1) * 128)
                        kb.P.op(DVE, lambda e, o=lP[:, jj, cs], d1=lgw[:, jj, cs]: e.tensor_tensor_scan(
                            out=o, data0=ones128[:], data1=d1, initial=0.0, op0=ALU.mult, op1=ALU.add), RP + [ones128B], [GB])
                kb.act(eP, lP, AF.Exp, RP, [GB])
                kb.tt(e2, lP, lgw, ALU.subtract, RP, [GB])
                kb.act(e2, e2, AF.Exp, RP, [GB])
                kb.act(e2b, lP, AF.Exp, RP, [GB], scale=-1.0)
                AR5 = AR.rearrange("p j (c a t) -> p j c a t", c=2, a=2)
                for jj in range(3):
                    kb.tt(AR5[:, jj, :, 0, :], a_[:, jj, :].rearrange("p (c t) -> p c t", c=2),
                          e2[:, jj, :].rearrange("p (c t) -> p c t", c=2), ALU.mult, RP, [GB])
                    kb.tt(AR5[:, jj, :, 1, :], rr[:, jj, :].rearrange("p (c t) -> p c t", c=2),
                          eP[:, jj, :].rearrange("p (c t) -> p c t", c=2), ALU.mult, RP, [GB])
                kb.tt(bT, b_, e2b, ALU.mult, RP, [GB])
                kb.tt(kT_, kmod, e2b, ALU.mult, RP, [GB])
                for jj in range(3):
                    for c in range(2):
                        cs = slice(c * 128, (c + 1) * 128)
                        pcol = eP[:, jj, c * 128 + 127:c * 128 + 128]
                        kb.ts(bc[:, jj, cs], bT[:, jj, cs], pcol, None, ALU.mult, None, RP, [GB])
                        kb.ts(kc[:, jj, cs], kT_[:, jj, cs], pcol, None, ALU.mult, None, RP, [GB])
                P.barrier()

                def core(jj, c):
                    I = INST[jj]
                    TM, AbT, AkT, Nm, RhT, Phi, Zz, Ytok = I["TM"], I["AbT"], I["AkT"], I["Nm"], I["RhT"], I["Phi"], I["Z"], I["Ytok"]
                    TMB, AbTB, AkTB, NmB, RhTB, PhiB, ZB, YtokB = (I[n_ + "B"] for n_ in ("TM", "AbT", "AkT", "Nm", "RhT", "Phi", "Z", "Ytok"))
                    NN = [I["NN0"], I["NN1"]]; NNB = [I["NN0B"], I["NN1B"]]
                    Rr = [I["R0"], I["R1"]]; RB = [I["R0B"], I["R1B"]]
                    cen, sqv, yn, st2, st3 = I["cen"], I["sqv"], I["yn"], I["st2"], I["st3"]
                    gnB = I["cenB"]
                    bk0 = int(os.environ.get("RWBANK", "4")) if jj == 2 else 2 * jj
                    bks = [kb.psb[bk0], kb.psb[bk0 + 1]]
                    ctr = [0]

                    def nb():
                        r_ = bks[ctr[0] % 2]
                        ctr[0] += 1
                        return r_
                    cs = slice(c * 128, (c + 1) * 128)
                    aT = AR5[:, jj, c, 0, :]
                    arT = AR5[:, jj, c, :, :].rearrange("p a t -> p (a t)")
                    pt, ptB = nb()
                    for i, srcT in enumerate((aT, vv[:, jj, cs], bc[:, jj, cs], kc[:, jj, cs])):
                        kb.tr(pt[:, i * 128:(i + 1) * 128], srcT, ident[:], [GB, identB], [ptB], inc=(i == 3))
                    kb.cp(TM.rearrange("p a b -> p (a b)"), pt[:], [ptB], [TMB], eng=ACT)
                    At = TM[:, 0, :]; Vt = TM[:, 1, :]; Bc = TM[:, 2, :]; Kc = TM[:, 3, :]
                    yield
                    if RWCUT <= 1:
                        return
                    mUf = mU[:].rearrange("p a t -> p (a t)")
                    pgs = [nb(), nb()]
                    for hh in range(2):
                        rows = slice(hh * 64, hh * 64 + 64)
                        pg, pgB = pgs[hh]
                        kb.mm(pg[:, 0:256], bT[rows, jj, cs], arT[rows, :], True, True, [GB], [pgB], inc=False)
                        kb.mm(pg[:, 256:512], kT_[rows, jj, cs], arT[rows, :], True, True, [GB], [pgB])
                    for hh in range(2):
                        pg, pgB = pgs[hh]
                        kb.tt(AbT[:, hh, :], pg[:, 0:256], mUf, ALU.mult, [pgB, mUB], [AbTB])
                        kb.tt(AkT[:, hh, :], pg[:, 256:512], mUf, ALU.mult, [pgB, mUB], [AkTB], eng=DVE)
                    AbT4 = AbT.rearrange("p h (a t) -> p h a t", a=2)
                    AkT4 = AkT.rearrange("p h (a t) -> p h a t", a=2)
                    yield
                    if RWCUT <= 2:
                        return
                    pm, pmB = nb()
                    for hh in range(2):
                        kb.tr(pm[:, hh * 128:(hh + 1) * 128], AbT4[:, hh, 0, :], ident[:], [AbTB, identB], [pmB], inc=False)
                    for hh in range(2):
                        kb.mm(pm[:, 256 + hh * 64:256 + (hh + 1) * 64], AkT4[:, hh, 0, :], Vt[:, hh * 64:(hh + 1) * 64], True, True,
                              [AkTB, TMB], [pmB], inc=(hh == 1))
                    kb.cp(Nm.rearrange("p h x -> p (h x)"), pm[:, 0:256], [pmB], [NmB], eng=ACT)
                    kb.cp(Rr[0][:, 1, :], pm[:, 256:384], [pmB], [RB[0]], eng=ACT)
                    kb.cp(Rr[0][:, 0, :], At, [TMB], [RB[0]], eng=POOL)
                    yield
                    if RWCUT <= 3:
                        return
                    for i in range(7):
                        if i == 0:
                            NTi = lambda hh: AbT4[:, hh, 0, :]
                            Ni = lambda hh: Nm[:, hh, :]
                            nB = [AbTB, NmB]
                        else:
                            nn = NN[i % 2]
                            NTi = lambda hh, nn=nn: nn[:, hh, :]
                            Ni = lambda hh, nn=nn: nn[:, 2 + hh, :]
                            nB = [NNB[i % 2]]
                        pr, prB = nb()
                        for hh in range(2):
                            kb.mm(pr[:, hh * 128:(hh + 1) * 128],
                                  NTi(hh), Rr[i % 2].rearrange("p a (h k) -> p a h k", h=2)[:, :, hh, :], True, True,
                                  nB + [RB[i % 2]], [prB], inc=(hh == 1))
                        if i < 6:
                            pn, pnB = nb()
                            for hh in range(2):
                                kb.mm(pn[:, hh * 128:(hh + 1) * 128], Ni(hh), NTi(hh), True, True, nB, [pnB], inc=False)
                                kb.mm(pn[:, 256 + hh * 128:256 + (hh + 1) * 128], NTi(hh), Ni(hh), True, True, nB, [pnB], inc=(hh == 1))
                        for hh in range(2):
                            kb.tt(Rr[(i + 1) % 2].rearrange("p a (h k) -> p a h k", h=2)[:, :, hh, :],
                                  pr[:, hh * 128:(hh + 1) * 128].rearrange("p (a k) -> p a k", a=2),
                                  Rr[i % 2].rearrange("p a (h k) -> p a h k", h=2)[:, :, hh, :],
                                  ALU.add, [prB, RB[i % 2]], [RB[(i + 1) % 2]])
                        if i < 6:
                            kb.cp(NN[(i + 1) % 2].rearrange("p a b -> p (a b)"), pn[:], [pnB], [NNB[(i + 1) % 2]], eng=ACT)
                        yield
                    if RWCUT <= 4:
                        return
                    Rf = Rr[1]; RfB = RB[1]
                    Ahat = Rf[:, 0, :]
                    W0 = Rf[:, 1, :]
                    pq, pqB = nb()
                    rt5 = AR5[:, jj, c, 1, :]
                    for hp in range(2):
                        kb.mm(pq[:, hp * 128:(hp + 1) * 128], Ahat, AbT4[:, hp, 1, :], True, True, [RfB, AbTB], [pqB], inc=False)
                    kb.mm(pq[:, 256:384], Ahat, Bc, True, True, [RfB, TMB], [pqB], inc=False)
                    kb.mm(pq[:, 384:512], Bc, W0, True, False, [TMB, RfB], [pqB], inc=False)
                    kb.mm(pq[:, 384:512], Kc, Vt, False, True, [TMB], [pqB])
                    for hp in range(2):
                        kb.tt(RhT[:, hp, :], pq[:, hp * 128:(hp + 1) * 128], rt5, ALU.add, [pqB, GB], [RhTB])
                        kb.ts(RhT[:, hp, :], RhT[:, hp, :], rowm[:, hp:hp + 1], None, ALU.mult, None, [RhTB, rowmB], [RhTB])
                    kb.tt(Phi, pq[:, 256:384], BD[:], ALU.mult, [pqB, BDB], [PhiB])
                    kb.stt(Phi, ident[:], eP[:, jj, c * 128 + 127:c * 128 + 128], Phi, ALU.mult, ALU.add, [identB, GB, PhiB], [PhiB])
                    kb.ts(Zz, pq[:, 384:448], rowm[:, 0:1], None, ALU.mult, None, [pqB, rowmB], [ZB])
                    kb.stt(Zz, pq[:, 448:512], rowm[:, 1:2], Zz, ALU.mult, ALU.add, [pqB, rowmB, ZB], [ZB])
                    yield
                    if RWCUT <= 5:
                        return
                    cur = scur[jj]
                    py, pyB = nb()
                    for hp in range(2):
                        osl = py[:, hp * 64:(hp + 1) * 64]
                        kb.mm(osl, AbT4[:, hp, 1, :], Rf[:, 1, hp * 64:(hp + 1) * 64], True, False, [AbTB, RfB], [pyB], inc=False)
                        kb.mm(osl, AkT4[:, hp, 1, :], Vt[:, hp * 64:(hp + 1) * 64], False, False, [AkTB, TMB], [pyB], inc=False)
                        kb.mm(osl, RhT[:, hp, :], ST[jj][cur], False, True, [RhTB, STB[jj][cur]], [pyB], inc=False)
                    kb.mm(py[:, 128:192], Phi, ST[jj][cur], True, True, [PhiB, STB[jj][cur]], [pyB])
                    kb.cp(YtA[:, c, jj * 128:(jj + 1) * 128], py[:, 0:128], [pyB], [YtAB], eng=ACT)
                    kb.tt(ST[jj][1 - cur], py[:, 128:192], Zz, ALU.add, [pyB, ZB], [STB[jj][1 - cur]])
                    scur[jj] = 1 - cur
                    yield
                    if not os.environ.get("RWOLDGN"):
                        return
                    yield "tail"
                    if RWCUT <= 6:
                        return
                    if os.environ.get("RWINST") and str(jj) not in os.environ["RWINST"]:
                        return
                    G = [YtokB, gnB, cstB]
                    kb.P.op(DVE, lambda e: e.reduce_sum(out=st2, in_=Ytok, axis=AX.X), G, [gnB])
                    kb.ts(st2, st2, -1.0 / 64, None, ALU.mult, None, G, [gnB])
                    kb.tt(cen, Ytok, st2.unsqueeze(2).to_broadcast([128, 2, 64]), ALU.add, G, [gnB])
                    kb.tt(sqv, cen, cen, ALU.mult, G, [gnB])
                    kb.P.op(DVE, lambda e: e.reduce_sum(out=st3, in_=sqv, axis=AX.X), G, [gnB])
                    kb.act(st3, st3, AF.Ln, G, [gnB], scale=1.0 / 64, bias=cst[:, 3:4])
                    if RWCUT <= 6.5:
                        return
                    kb.act(st3, st3, AF.Exp, G, [gnB], scale=-0.5)
                    if RWCUT <= 6.6:
                        return
                    kb.tt(yn, cen, st3.unsqueeze(2).to_broadcast([128, 2, 64]), ALU.mult, G, [gnB])
                    if RWCUT <= 6.7:
                        return
                    pt2, pt2B = nb()
                    kb.tr(pt2[:, 0:128], yn.rearrange("p h v -> p (h v)"), ident[:], [gnB, identB], [pt2B])
                    kb.ts(yfm[:, jj, cs], pt2[:, 0:128], rwp[:, l, 25 + jj:26 + jj], rwp[:, l, 28 + jj:29 + jj],
                          ALU.mult, ALU.add, [pt2B, rwpB, GB], [I["ynB"]])
                    yield

                for c in range(2):
                    gens = [core(jj, c) for jj in range(2 if os.environ.get("RW2") else 3)]
                    live = list(gens)
                    if os.environ.get("RWSEQ"):
                        for g_i in gens:
                            for _ in g_i:
                                pass
                        live = []
                    tails = []
                    while live:
                        for g_i in list(live):
                            try:
                                if next(g_i) == "tail":
                                    live.remove(g_i)
                                    tails.append(g_i)
                            except StopIteration:
                                live.remove(g_i)
                    if tails:
                        P.barrier()
                    for g_i in tails:
                        for _ in g_i:
                            pass
                    if os.environ.get("RW2") and not os.environ.get("RWSEQ"):
                        for _ in core(2, c):
                            pass
                P.barrier()
                if RWCUT <= 7:
                    continue
                if not os.environ.get("RWOLDGN"):
                    cenA = arena[:, o_core:o_core + 768].rearrange("p (n v) -> p n v", v=64)
                    sqA = arena[:, o_core + 768:o_core + 1536].rearrange("p (n v) -> p n v", v=64)
                    stA = arena[:, o_core + 1536:o_core + 1548]
                    stB = arena[:, o_core + 1552:o_core + 1564]
                    gA = Buf("gnA")
                    Yv = YtA.rearrange("p c (n v) -> p (c n) v", v=64)
                    GA = [YtAB, gA, cstB]
                    kb.P.op(DVE, lambda e: e.reduce_sum(out=stA, in_=Yv, axis=AX.X), GA, [gA])
                    kb.ts(stA, stA, -1.0 / 64, None, ALU.mult, None, GA, [gA])
                    kb.tt(cenA, Yv, stA.unsqueeze(2).to_broadcast([128, 12, 64]), ALU.add, GA, [gA])
                    kb.tt(sqA, cenA, cenA, ALU.mult, GA, [gA])
                    kb.P.op(DVE, lambda e: e.reduce_sum(out=stB, in_=sqA, axis=AX.X), GA, [gA])
                    kb.act(stB, stB, AF.Ln, GA, [gA], scale=1.0 / 64, bias=cst[:, 3:4])
                    kb.act(stB, stB, AF.Exp, GA, [gA], scale=-0.5)
                    kb.tt(cenA, cenA, stB.unsqueeze(2).to_broadcast([128, 12, 64]), ALU.mult, GA, [gA])
                    cen2 = cenA.rearrange("p (c j h) v -> p c j (h v)", c=2, j=3)
                    for c in range(2):
                        ptg, ptgB = kb.psb[6 + c]
                        for jj in range(3):
                            kb.tr(ptg[:, jj * 128:(jj + 1) * 128], cen2[:, c, jj, :], ident[:], [gA, identB], [ptgB], inc=(jj == 2))
                        ysl = yfm[:, :, c * 128:(c + 1) * 128]
                        kb.tt(ysl, ptg[:, 0:384].rearrange("p (j t) -> p j t", j=3),
                              rwp[:, l, 25:28].unsqueeze(2).to_broadcast([128, 3, 128]), ALU.mult, [ptgB, rwpB, GB], [GB])
                        kb.tt(ysl, ysl, rwp[:, l, 28:31].unsqueeze(2).to_broadcast([128, 3, 128]), ALU.add, [rwpB, GB], [GB])
                kb.tt(yfm, yfm, bonus, ALU.add, RP, [GB])
                kb.tt(ycat[:, 3:6, t0:t0 + 256], yfm, g_, ALU.mult, RP, [ycatB[3], ycatB[4], ycatB[5]])

        def mixer_ssm(l):
            cm.off = MB1
            bbj = cm.f32([2, 128]); bbB = Buf("bbj")
            ccj = cm.f32([2, 128]); ccB = Buf("ccj")
            uT = cm.f32([2, T]); uTB = [Buf("uT0"), Buf("uT1")]
            yacc = cm.f32([2, T]); yaccB = [Buf("yacc0"), Buf("yacc1")]
            tab = cm.f32([4, 512]); tabB = Buf("tab")
            rhoj = cm.f32([512]); rhojB = Buf("rhoj")
            m = [[cm.f32([512]) for i in range(4)] for u_ in range(2)]
            mB = [[Buf("m%d%d" % (u_, i)) for i in range(4)] for u_ in range(2)]
            bpr = [cm.f32([512]) for u_ in range(2)]; bpi = [cm.f32([512]) for u_ in range(2)]
            bpB = [Buf("bp0"), Buf("bp1")]
            wre = [cm.f32([512]) for u_ in range(2)]; wim = [cm.f32([512]) for u_ in range(2)]
            wB_ = [Buf("w0"), Buf("w1")]
            xr = [cm.f32([512]) for i in range(2)]; xi = [cm.f32([512]) for i in range(2)]
            xB = [Buf("x0"), Buf("x1")]
            car = [cm.f32([4]) for u_ in range(2)]; carB = [Buf("car0"), Buf("car1")]
            zero1 = cm.f32([1]); zB = Buf("zero1")
            sgm = cm.f32([512]); sgmB = Buf("sgm")
            kb.memset(zero1, 0.0, [zB])
            for c in range(2):
                P.dma(SP, uT[:, c, :], pT[15 + c], reads=[pTB[15 + c]], writes=[uTB[c]])
                kb.ts(yacc[:, c, :], uT[:, c, :], s5v[:, l, 0, c:c + 1], None, ALU.mult, None, [uTB[c], s5vB], [yaccB[c]])
            Ec, Es, Tc, Tn = (tab[:, a, :] for a in range(4))

            def stA(u):
                j, tt = u // 4, u % 4
                c = j // 4
                t0 = tt * 512
                ub = u % 2
                if tt == 0:
                    P.dma(SP, tab, tabs[l, j], writes=[tabB])
                    P.dma(SP, bbj, bb_d[:, l, :, j, :], writes=[bbB])
                    P.dma(SP, ccj, cc_d[:, l, :, j, :], writes=[ccB])
                    kb.ts(ccj[:, 1, :], ccj[:, 1, :], -1.0, None, ALU.mult, None, [ccB], [ccB])
                    kb.cp(rhoj, rho[:, l, j:j + 1].to_broadcast([128, 512]), [rhoB], [rhojB])
                pR, pRB = kb.psb[1 + ub]
                pI, pIB = kb.psb[3 + ub]
                kb.mm(pR[:], bbj[:, 0, :], uT[:, c, t0:t0 + 512], True, True, [bbB, uTB[c]], [pRB])
                kb.mm(pI[:], bbj[:, 1, :], uT[:, c, t0:t0 + 512], True, True, [bbB, uTB[c]], [pIB])
                mm_, mmB = m[ub], mB[ub]
                kb.tt(mm_[0], pR[:], Tc, ALU.mult, [pRB, tabB], [mmB[0]])
                kb.tt(mm_[1], pI[:], Tn, ALU.mult, [pIB, tabB], [mmB[1]])
                kb.tt(mm_[2], pR[:], Tn, ALU.mult, [pRB, tabB], [mmB[2]])
                kb.tt(mm_[3], pI[:], Tc, ALU.mult, [pIB, tabB], [mmB[3]])
                kb.tt(bpr[ub], mm_[0], mm_[1], ALU.subtract, [mmB[0], mmB[1]], [bpB[ub]], eng=POOL)
                kb.tt(bpi[ub], mm_[2], mm_[3], ALU.add, [mmB[2], mmB[3]], [bpB[ub]], eng=POOL)

            def stB(u):
                tt = u % 4
                ub = u % 2
                if tt == 0:
                    inr, ini, inB = zero1, zero1, zB
                else:
                    inr, ini, inB = car[1 - ub][:, 0:1], car[1 - ub][:, 1:2], carB[1 - ub]
                kb.P.op(DVE, lambda e, o=wre[ub], d1=bpr[ub], i0=inr: e.tensor_tensor_scan(
                    out=o, data0=rhoj, data1=d1, initial=i0, op0=ALU.mult, op1=ALU.add), [rhojB, bpB[ub], inB], [wB_[ub]])
                kb.P.op(DVE, lambda e, o=wim[ub], d1=bpi[ub], i0=ini: e.tensor_tensor_scan(
                    out=o, data0=rhoj, data1=d1, initial=i0, op0=ALU.mult, op1=ALU.add), [rhojB, bpB[ub], inB], [wB_[ub]])
                if tt < 3:
                    wl_r, wl_i = wre[ub][:, 511:512], wim[ub][:, 511:512]
                    ec_l, es_l = Ec[:, 511:512], Es[:, 511:512]
                    cr = car[ub]
                    RC = [wB_[ub], tabB, carB[ub]]
                    kb.tt(cr[:, 2:3], wl_i, es_l, ALU.mult, RC, [carB[ub]])
                    kb.tt(cr[:, 3:4], wl_i, ec_l, ALU.mult, RC, [carB[ub]])
                    kb.stt(cr[:, 0:1], wl_r, ec_l, cr[:, 2:3], ALU.mult, ALU.subtract, RC, [carB[ub]])
                    kb.stt(cr[:, 1:2], wl_r, es_l, cr[:, 3:4], ALU.mult, ALU.add, RC, [carB[ub]])

            def stC(u):
                j, tt = u // 4, u % 4
                ub = u % 2
                mm_, mmB = m[ub], mB[ub]
                kb.tt(mm_[0], wre[ub], Ec, ALU.mult, [wB_[ub], tabB], [mmB[0]])
                kb.tt(mm_[1], wim[ub], Es, ALU.mult, [wB_[ub], tabB], [mmB[1]])
                kb.tt(mm_[2], wre[ub], Es, ALU.mult, [wB_[ub], tabB], [mmB[2]])
                kb.tt(mm_[3], wim[ub], Ec, ALU.mult, [wB_[ub], tabB], [mmB[3]])
                kb.tt(xr[ub], mm_[0], mm_[1], ALU.subtract, [mmB[0], mmB[1]], [xB[ub]], eng=POOL)
                kb.tt(xi[ub], mm_[2], mm_[3], ALU.add, [mmB[2], mmB[3]], [xB[ub]], eng=POOL)
                pY, pYB = kb.psb[5 + ub]
                kb.mm(pY[:], ccj[:, 0, :], xr[ub], True, False, [ccB, xB[ub]], [pYB], inc=False)
                kb.mm(pY[:], ccj[:, 1, :], xi[ub], False, True, [ccB, xB[ub]], [pYB])

            def stD(u):
                j, tt = u // 4, u % 4
                c = j // 4
                t0 = tt * 512
                pY, pYB = kb.psb[5 + u % 2]
                kb.tt(yacc[:, c, t0:t0 + 512], yacc[:, c, t0:t0 + 512], pY[:], ALU.add, [yaccB[c], pYB], [yaccB[c]])

            NU = 32
            stA(0)
            for u in range(NU):
                stB(u)
                stC(u)
                if u + 1 < NU and (u + 1) % 4 != 0:
                    stA(u + 1)
                if u >= 1:
                    stD(u - 1)
                if u + 1 < NU and (u + 1) % 4 == 0:
                    stA(u + 1)
            stD(NU - 1)
            k = 0
            for c in range(2):
                kb.act(yacc[:, c, :], yacc[:, c, :], AF.Gelu, [yaccB[c]], [yaccB[c]])
            for oc2 in range(2):
                for tt in range(4):
                    t0 = tt * 512
                    ps, psB = kb.psb[1 + k % 2]
                    k += 1
                    for c2 in range(2):
                        kb.mm(ps[:], gluw[:, l, oc2, c2, :], yacc[:, c2, t0:t0 + 512], c2 == 0, c2 == 1,
                              [gluwB, yaccB[c2]], [psB], inc=(c2 == 1))
                    kb.act(sgm, ps[:], AF.Sigmoid, [psB, s5vB], [sgmB], bias=s5v[:, l, 1, oc2:oc2 + 1])
                    kb.tt(ycat[:, 6 + oc2, t0:t0 + 512], yacc[:, oc2, t0:t0 + 512], sgm, ALU.mult,
                          [yaccB[oc2], sgmB], [ycatB[6 + oc2]])

        def mixer_outproj(l):
            cm.off = MB0
            msq = cm.bf16([NCH, 512]); msqB = Buf("msq")
            mr = cm.f32([512]); mrB = Buf("mr")
            mr0 = cm.f32([512]); mr0B = Buf("mr0")
            wsl = [cm.bf16([D]) for i in range(3)]; wslB = [Buf("wsl%d" % i) for i in range(3)]
            mo = cm.f32([NCH, 512]); moB = Buf("mo")
            tm = [cm.f32([512]) for i in range(2)]; tmB = [Buf("tm%d" % i) for i in range(2)]
            gpost = 6 * l + 3
            k = 0
            for tt in range(4):
                t0 = tt * 512
                for dc in range(NCH):
                    wi = k % 3
                    P.dma(SP, wsl[wi], wout_b[l, dc * 128:(dc + 1) * 128, :], reads=[convB], writes=[wslB[wi]])
                    ps, psB = kb.psb[1 + k % 4]
                    k += 1
                    for c in range(NCH):
                        kb.mm(ps[:], wsl[wi][:, c * 128:(c + 1) * 128], ycat[:, c, t0:t0 + 512], c == 0, c == NCH - 1,
                              [wslB[wi], ycatB[c]], [psB], inc=(c == NCH - 1))
                    kb.cp(mo[:, dc, :], ps[:], [psB], [moB], eng=ACT)
                    kb.act(msq[:, dc, :], ps[:], AF.Square, [psB], [msqB])
                rms_rstd(lambda c: msq[:, c, :], msqB, mr, mrB, mr0, mr0B)
                for c in range(NCH):
                    ti = c % 2
                    kb.stt(tm[ti], mo[:, c, :], g32[:, gpost, c:c + 1], mr, ALU.mult, ALU.mult,
                           [moB, g32B, mrB], [tmB[ti]])
                    kb.tt(h[:, c, t0:t0 + 512], h[:, c, t0:t0 + 512], tm[ti], ALU.add, [hB[tt], tmB[ti]], [hB[tt]], eng=POOL)

        def mixer(l, s):
            P.barrier()
            if "inproj" in parts:
                mixer_inproj(l)
                P.barrier()
            if "attn" in parts:
                mixer_attn(l)
                P.barrier()
            if "attn" not in parts:
                kb.memset(ycat[:, 0:3, :], 0.0, [ycatB[0], ycatB[1], ycatB[2]])
            if "ssm" not in parts:
                kb.memset(ycat[:, 6:8, :], 0.0, [ycatB[6], ycatB[7]])
            import os as _os
            kb.memset(ycat[:, 3:6, :], float("nan") if _os.environ.get("RWNAN") else 0.0, [ycatB[3], ycatB[4], ycatB[5]])
            if "ssm" in parts:
                mixer_ssm(l)
                P.barrier()
            if "rwkv" in parts:
                mixer_rwkv(l)
                P.barrier()
            if debug == "ycat%d" % l:
                cm.off = MB0
                dstg = [cm.f32([T]) for i in range(2)]
                dB = [Buf("d0"), Buf("d1")]
                for c in range(NCH):
                    kb.cp(dstg[c % 2], ycat[:, c, :], [ycatB[c]], [dB[c % 2]])
                    P.dma(SP, dbg[s, c], dstg[c % 2], reads=[dB[c % 2]], final=True)
                P.barrier()
            if "outproj" in parts:
                mixer_outproj(l)
                P.barrier()

        for s in range(nseq):
            for c in range(NCH):
                P.dma(SP, h[:, c, :], xT[s, c], writes=hB)
            P.barrier()
            phase = 0
            for l in range(n_layers):
                if phase < stop_after and "ffn" in parts:
                    ffn(2 * l, 6 * l + 0, 6 * l + 1)
                phase += 1
                if phase < stop_after:
                    mixer(l, s)
                phase += 1
                if phase < stop_after and "ffn" in parts:
                    ffn(2 * l + 1, 6 * l + 4, 6 * l + 5)
                phase += 1
            for c in range(NCH):
                P.dma(SP, outT[s, c], h[:, c, :], reads=hB, final=True)
        P.emit()
    return nc


def _buckets():
    W = 128
    qi = np.arange(W)[:, None]
    kj = np.arange(2 * W)[None, :]
    rel = qi + W - kj
    inw = (rel >= 0) & (rel < W)
    n = np.maximum(rel, 0)
    nf = np.maximum(n, 1).astype(np.float32)
    large = 16 + (np.log(nf / 16) / np.float32(np.log(128 / 16)) * 16).astype(np.int32)
    large = np.minimum(large, 31)
    return np.where(n < 16, n, large), inw


def host_layout(inputs):
    L = 2
    wgu = np.empty((4, 2, DFF, D), np.float32)
    wdn = np.empty((4, D, DFF), np.float32)
    ffw = {"ffn1": (inputs["ffn1_w_gate"], inputs["ffn1_w_up"], inputs["ffn1_w_down"]),
           "ffn2": (inputs["ffn2_w_gate"], inputs["ffn2_w_up"], inputs["ffn2_w_down"])}
    for l in range(L):
        for j, nm in enumerate(("ffn1", "ffn2")):
            fi = 2 * l + j
            for m in range(2):
                W = ffw[nm][m][l]
                wgu[fi, m] = W.reshape(NCH, 128, NF, 128).transpose(2, 1, 0, 3).reshape(DFF, D)
            Wd = ffw[nm][2][l]
            wdn[fi] = Wd.reshape(NF, 128, NCH, 128).transpose(2, 1, 0, 3).reshape(D, DFF)
    gl = []
    for l in range(L):
        for nm in ("ln_pre_ffn1", "ln_post_ffn1", "ln_pre_mix", "ln_post_mix", "ln_pre_ffn2", "ln_post_ffn2"):
            gl.append(inputs[nm][l].reshape(NCH, 128).T)
    gains = np.ascontiguousarray(np.stack(gl, axis=1)).astype(np.float32)
    qperm = []
    for g in range(3):
        qperm += list(range(g * 64, (g + 1) * 64)) + list(range((3 + g) * 64, (4 + g) * 64))
    cols = np.array(qperm + list(range(384, 2176)))
    win = np.empty((L, NOC * 128, D), np.float32)
    wout = np.empty((L, D, D), np.float32)
    for l in range(L):
        Wp = inputs["w_in"][l][:, cols]
        win[l] = Wp.reshape(NCH, 128, NOC, 128).transpose(2, 1, 0, 3).reshape(NOC * 128, D)
        wout[l] = inputs["w_out"][l].reshape(NCH, 128, NCH, 128).transpose(2, 1, 0, 3).reshape(D, D)
    bucket, inw = _buckets()
    rb = inputs["rel_bias"]
    bt = rb[bucket]
    biasT = np.ascontiguousarray(bt.reshape(128, 2, 128, 6).transpose(2, 1, 3, 0)).astype(np.float32)
    maskT = np.where(inw.reshape(128, 2, 128).transpose(2, 1, 0), 0.0, -30000.0).astype(np.float32)
    sinks = np.ascontiguousarray(np.broadcast_to(inputs["att_sinks"][None], (128, L, 6))).astype(np.float32)
    s5p = np.zeros((128, L, 3, 8), np.float32)
    bb = np.zeros((128, L, 2, 8, 128), np.float32)
    cc = np.zeros((128, L, 2, 8, 128), np.float32)
    s5v = np.zeros((128, L, 2, 2), np.float32)
    gluw = np.zeros((128, L, 2, 2, 128), np.float32)
    for l in range(L):
        for j in range(8):
            for g2 in range(2):
                g = 2 * j + g2
                ps_ = slice(g2 * 64, (g2 + 1) * 64)
                s5p[ps_, l, 0, j] = inputs["ssm_lambda_re"][l, g]
                s5p[ps_, l, 1, j] = inputs["ssm_lambda_im"][l, g]
                s5p[ps_, l, 2, j] = inputs["ssm_log_dt"][l, g]
                q0 = (g % 8) * 16
                bb[q0:q0 + 16, l, 0, j, ps_] = inputs["ssm_b_re"][l, g].T
                bb[q0:q0 + 16, l, 1, j, ps_] = inputs["ssm_b_im"][l, g].T
                cc[ps_, l, 0, j, q0:q0 + 16] = inputs["ssm_c_re"][l, g].T
                cc[ps_, l, 1, j, q0:q0 + 16] = inputs["ssm_c_im"][l, g].T
        s5v[:, l, 0, :] = inputs["ssm_d"][l].reshape(2, 128).T
        s5v[:, l, 1, :] = inputs["ssm_glu_b"][l].reshape(2, 128).T
        gluw[:, l] = inputs["ssm_glu_w"][l].reshape(2, 128, 2, 128).transpose(1, 2, 0, 3)
    rwp = np.zeros((128, L, 31), np.float32)
    lw = np.zeros((128, L, 384), np.float32)
    for l in range(L):
        rwp[:, l, 0:10] = inputs["rwkv_mu"][l].reshape(10, 128).T
        for i, nm in enumerate(("rwkv_w0", "rwkv_a0", "rwkv_k_k", "rwkv_k_a", "rwkv_r_k", "rwkv_gn_w", "rwkv_gn_b")):
            rwp[:, l, 10 + 3 * i:13 + 3 * i] = inputs[nm][l].reshape(3, 128).T
        lw[0:32, l] = inputs["rwkv_w2"][l]
        lw[32:64, l] = inputs["rwkv_a2"][l]
        lw[64:128, l] = inputs["rwkv_g2"][l]
    ii = np.arange(128)
    mU = np.stack([(ii[:, None] < ii[None, :]), (ii[:, None] <= ii[None, :])], axis=1).astype(np.float32)
    mL = (ii[None, :] < ii[:, None]).astype(np.float32)
    BDm = (ii[:, None] // 64 == ii[None, :] // 64).astype(np.float32)
    rowm = np.stack([(ii < 64), (ii >= 64)], axis=1).astype(np.float32)
    return {"rwp": rwp, "lw": lw, "mU": mU, "mL": mL, "BD": BDm, "rowm": rowm, "s5p": s5p, "bb": bb, "cc": cc, "s5v": s5v, "gluw": gluw,
            "wgu": wgu, "wdn": wdn, "gains": gains, "win": win, "wout": wout,
            "ident": np.eye(128, dtype=np.float32), "biasT": biasT, "maskT": maskT, "sinks": sinks}


def run(inputs, nseq, ncores, stop_after=99, debug=None, parts=ALL_PARTS):
    shared = host_layout(inputs)
    x = inputs["x"]
    nc = build(nseq, stop_after, debug=debug, parts=parts)
    in_maps = []
    for c in range(ncores):
        xs = x[c * nseq:(c + 1) * nseq]
        xt = np.ascontiguousarray(xs.transpose(0, 2, 1)).reshape(nseq, NCH, 128, T)
        m = dict(shared)
        m["xT"] = xt
        in_maps.append(m)
    res = run_bass_kernel_spmd(nc, in_maps, core_ids=list(range(ncores)))
    outs = []
    dbgs = []
    for c in range(ncores):
        outs.append(res.results[c]["outT"].reshape(nseq, D, T).transpose(0, 2, 1))
        if debug is not None:
            dbgs.append(res.results[c]["dbg"].reshape(nseq, D, T).transpose(0, 2, 1))
    out = np.ascontiguousarray(np.concatenate(outs, axis=0)).astype(np.float32)
    if debug is not None:
        return out, np.concatenate(dbgs, axis=0)
    return out


def kernel(**inputs):
    inputs = {k: np.asarray(v) for k, v in inputs.items()}
    return run(inputs, 4, 8)
```
